# Optimizing a Trainium2 kernel written in Bass

```python
import math
import jax, jax.numpy as jnp
from jax import lax
import numpy as np

D_MODEL = 2048
BATCH = 2
SEQ = 4096
DEPTH = 1

HEAD_DIM = 128
ATTN_WIDTH = D_MODEL // 2
N_ATTN_HEADS = ATTN_WIDTH // HEAD_DIM
CONV_WIDTH = D_MODEL - ATTN_WIDTH
CONV_K = 3
IN_WIDTH = 3 * ATTN_WIDTH + 3 * CONV_WIDTH
D_FF = ((8 * D_MODEL // 3 + 255) // 256) * 256
N_MOD = 9
Q_BLOCK = 128
FFN_RES = 0.5
EPS = 1e-6

kernel_name = "hymba_stickbreak_shortconv_macaron_adaln"


def rms_norm(h):
    hf = h.astype(jnp.float32)
    hf = hf * lax.rsqrt(jnp.mean(hf * hf, axis=-1, keepdims=True) + EPS)
    return hf.astype(h.dtype)


def rms_norm_gain(h, gain):
    return rms_norm(h) * gain


def modulate(h, shift, scale):
    return h * (1.0 + scale[:, None, :]) + shift[:, None, :]


def swiglu(h, w_gu, w_down):
    gate, up = jnp.split(h @ w_gu, 2, axis=-1)
    return (jax.nn.silu(gate) * up) @ w_down


def stick_breaking_attention(q, k, v):
    seq = q.shape[2]
    inv_sqrt_d = 1.0 / math.sqrt(q.shape[-1])
    qf = q.astype(jnp.float32)
    kf = k.astype(jnp.float32)
    outs = []
    for b in range(seq // Q_BLOCK):
        t0, t1 = b * Q_BLOCK, (b + 1) * Q_BLOCK
        z = jnp.einsum('bhtd,bhsd->bhts', qf[:, :, t0:t1], kf[:, :, :t1]) * inv_sqrt_d
        t_idx = t0 + jnp.arange(Q_BLOCK)[:, None]
        s_idx = jnp.arange(t1)[None, :]
        causal = s_idx < t_idx
        log_fail = jnp.where(causal, jax.nn.log_sigmoid(-z), 0.0)
        log_tail = lax.cumsum(log_fail, axis=3, reverse=True) - log_fail
        a = jnp.where(causal, jnp.exp(jax.nn.log_sigmoid(z) + log_tail), 0.0)
        outs.append(jnp.einsum('bhts,bhsd->bhtd', a.astype(v.dtype), v[:, :, :t1]))
    return jnp.concatenate(outs, axis=2)


def causal_short_conv(u, w):
    ch = u.shape[-1]
    return lax.conv_general_dilated(
        u, w[:, None, :], window_strides=(1,), padding=[(CONV_K - 1, 0)],
        dimension_numbers=('NWC', 'WIO', 'NWC'), feature_group_count=ch)


def hybrid_mixer(h, w_in, q_norm_w, k_norm_w, conv_w, w_out):
    bsz, seq, _ = h.shape
    proj = h @ w_in
    offs = np.cumsum([ATTN_WIDTH, ATTN_WIDTH, ATTN_WIDTH, CONV_WIDTH, CONV_WIDTH])
    q, k, v, gate_b, gate_c, u = jnp.split(proj, list(offs), axis=-1)
    def heads(t):
        return t.reshape(bsz, seq, N_ATTN_HEADS, HEAD_DIM)
    q = rms_norm_gain(heads(q), q_norm_w).transpose(0, 2, 1, 3)
    k = rms_norm_gain(heads(k), k_norm_w).transpose(0, 2, 1, 3)
    v = heads(v).transpose(0, 2, 1, 3)
    attn = stick_breaking_attention(q, k, v).transpose(0, 2, 1, 3).reshape(bsz, seq, ATTN_WIDTH)
    conv = gate_b * causal_short_conv(gate_c * u, conv_w)
    return jnp.concatenate([attn, conv], axis=-1) @ w_out


def setup_inputs(seed: int = 0) -> dict:
    key = jax.random.key(seed)
    ks = jax.random.split(key, 13)
    f32 = jnp.float32
    def nrm(k, shape, scale):
        return jax.random.normal(k, shape, f32) * scale
    return {
        "x": nrm(ks[0], (BATCH, SEQ, D_MODEL), 1.0),
        "c": nrm(ks[1], (BATCH, D_MODEL), 1.0),
        "w_ada": nrm(ks[2], (DEPTH, D_MODEL, N_MOD * D_MODEL), 0.5 * D_MODEL ** -0.5),
        "b_ada": nrm(ks[3], (DEPTH, N_MOD * D_MODEL), 0.02),
        "w1_gu": nrm(ks[4], (DEPTH, D_MODEL, 2 * D_FF), D_MODEL ** -0.5),
        "w1_down": nrm(ks[5], (DEPTH, D_FF, D_MODEL), D_FF ** -0.5),
        "w_in": nrm(ks[6], (DEPTH, D_MODEL, IN_WIDTH), D_MODEL ** -0.5),
        "q_norm_w": 1.0 + nrm(ks[7], (DEPTH, HEAD_DIM), 0.02),
        "k_norm_w": 1.0 + nrm(ks[8], (DEPTH, HEAD_DIM), 0.02),
        "conv_w": nrm(ks[9], (DEPTH, CONV_K, CONV_WIDTH), CONV_K ** -0.5),
        "w_out": nrm(ks[10], (DEPTH, D_MODEL, D_MODEL), D_MODEL ** -0.5),
        "w2_gu": nrm(ks[11], (DEPTH, D_MODEL, 2 * D_FF), D_MODEL ** -0.5),
        "w2_down": nrm(ks[12], (DEPTH, D_FF, D_MODEL), D_FF ** -0.5),
    }


def reference(x, c, w_ada, b_ada, w1_gu, w1_down, w_in, q_norm_w, k_norm_w,
              conv_w, w_out, w2_gu, w2_down):
    c_act = jax.nn.silu(c)
    for l in range(DEPTH):
        mod = c_act @ w_ada[l] + b_ada[l]
        (sh1, sc1, g1, sh2, sc2, g2, sh3, sc3, g3) = jnp.split(mod, N_MOD, axis=-1)
        h = modulate(rms_norm(x), sh1, sc1)
        x = x + FFN_RES * g1[:, None, :] * swiglu(h, w1_gu[l], w1_down[l])
        h = modulate(rms_norm(x), sh2, sc2)
        x = x + g2[:, None, :] * hybrid_mixer(h, w_in[l], q_norm_w[l], k_norm_w[l], conv_w[l], w_out[l])
        h = modulate(rms_norm(x), sh3, sc3)
        x = x + FFN_RES * g3[:, None, :] * swiglu(h, w2_gu[l], w2_down[l])
    return x
```

```python
import numpy as np
import ml_dtypes
from contextlib import ExitStack

import concourse.bass as bass
import concourse.mybir as mybir
from concourse.bass_utils import run_bass_kernel_spmd

F32 = mybir.dt.float32
BF16 = mybir.dt.bfloat16
AF = mybir.ActivationFunctionType
ALU = mybir.AluOpType

D = 2048
DFF = 5632
NT = 1024
SEQ = 4096
NFT = 16
EPS = 1e-6
G4 = [[0, 1, 2, 3], [4, 5, 6, 7]]
G8 = [list(range(8))]
ENGS = ["pe", "act", "dve", "pool", "sp"]
NSLOT = 7
STAGE = 99
DO_FFN1 = True
MIX_STOP = 99


class V:
    def __init__(self, ap, keys):
        self.ap = ap
        self.keys = keys


class Reg:
    def __init__(self, name, t, n, cell):
        self.name, self.t, self.n, self.cell = name, t, n, cell

    def keys(self, lo, hi):
        return [(self.name, c) for c in range(lo // self.cell, (hi - 1) // self.cell + 1)]

    def v(self, lo, hi, pat=None, p0=0, p1=128, **kw):
        ap = self.t[p0:p1, lo:hi]
        if pat is not None:
            ap = ap.rearrange(pat, **kw)
        return V(ap, self.keys(lo, hi))


class KB:
    def __init__(self, nc, es):
        self.nc, self.es = nc, es
        self.q = {e: [] for e in ENGS}
        self.cnt = {e: 0 for e in ENGS}
        self.pending = {e: False for e in ENGS}
        self.semh = {}
        self.waited = {e: {} for e in ENGS}
        self.cells = {}
        self.tr = {e: [] for e in ENGS}
        self.dma_i = {"sp": 0, "pool": 0, "act": 0}
        self.NDS = 8
        self.ncc = 0
        for e in ENGS:
            self.semh[e] = es.enter_context(nc.semaphore("s_" + e))
        for e in ("sp", "pool"):
            for i in range(self.NDS):
                self.semh[("d", e, i)] = es.enter_context(nc.semaphore("d_%s_%d" % (e, i)))

    def _deps(self, eng, reads, writes):
        deps = {}

        def need(ev):
            if ev is None:
                return
            sk, val = ev
            if eng == "pe" and sk == "pe":
                return
            if deps.get(sk, 0) < val:
                deps[sk] = val

        for v in reads:
            for k in v.keys:
                c = self.cells.get(k)
                if c:
                    need(c[0])
        for v in writes:
            for k in v.keys:
                c = self.cells.get(k)
                if c:
                    need(c[0])
                    for sk, val in c[1].items():
                        need((sk, val))
        return deps

    def _emit_waits(self, eng, deps):
        for sk, val in deps.items():
            if self.waited[eng].get(sk, 0) >= val:
                continue
            self.waited[eng][sk] = val
            sem = self.semh[sk]
            self.tr[eng].append(("w", sk, val))
            self.q[eng].append(lambda e, sem=sem, val=val: e.wait_ge(sem, val))

    def _record(self, ev, reads, writes):
        for v in writes:
            for k in v.keys:
                self.cells[k] = [ev, {}]
        for v in reads:
            for k in v.keys:
                c = self.cells.setdefault(k, [None, {}])
                if c[1].get(ev[0], 0) < ev[1]:
                    c[1][ev[0]] = ev[1]

    def op(self, eng, fn, reads=(), writes=(), signal=True):
        self._emit_waits(eng, self._deps(eng, reads, writes))
        if signal:
            self.cnt[eng] += 1
            ev = (eng, self.cnt[eng])
            sem = self.semh[eng]
            self.q[eng].append(lambda e, fn=fn, sem=sem: fn(e).then_inc(sem, 1))
            self.tr[eng].append(("i", eng, 1))
            self.pending[eng] = False
        else:
            ev = (eng, self.cnt[eng] + 1)
            self.q[eng].append(lambda e, fn=fn: fn(e))
            self.pending[eng] = True
        self._record(ev, reads, writes)
        return ev

    def dma(self, qe, out, in_, reads=None, writes=None):
        reads = [in_] if reads is None else reads
        writes = [out] if writes is None else writes
        deps = self._deps(qe, reads, writes)
        i = self.dma_i[qe]
        self.dma_i[qe] += 1
        sk = ("d", qe, i % self.NDS)
        val = 16 * (i // self.NDS + 1)
        if i >= self.NDS and deps.get(sk, 0) < val - 16:
            deps[sk] = val - 16
        self._emit_waits(qe, deps)
        sem = self.semh[sk]
        oap, iap = out.ap, in_.ap

        def f(e, oap=oap, iap=iap, sem=sem):
            o = oap(e) if callable(oap) else oap
            s = iap(e) if callable(iap) else iap
            try:
                e.dma_start(out=o, in_=s).then_inc(sem, 16)
            except Exception:
                print("DMA FAIL", o, s)
                raise

        self.q[qe].append(f)
        self.tr[qe].append(("i", sk, 16))
        ev = (sk, val)
        self._record(ev, reads, writes)
        return ev

    def cc(self, in_v, out_v, groups):
        deps = self._deps("pool", [in_v], [out_v])
        self._emit_waits("pool", deps)
        sk = ("cc", self.ncc)
        self.ncc += 1
        self.semh[sk] = self.es.enter_context(self.nc.semaphore("cc%d" % self.ncc))
        sem = self.semh[sk]
        iap, oap = in_v.ap, out_v.ap
        self.q["pool"].append(
            lambda e: e.collective_compute(
                "AllGather", ALU.bypass, replica_groups=groups, ins=[iap.opt()], outs=[oap.opt()]
            ).then_inc(sem)
        )
        self.tr["pool"].append(("i", sk, 1))
        ev = (sk, 1)
        self._record(ev, [in_v], [out_v])
        return ev

    def check_deadlock(self):
        pos = {e: 0 for e in ENGS}
        sem = {}
        progress = True
        while progress:
            progress = False
            for e in ENGS:
                tr = self.tr[e]
                while pos[e] < len(tr):
                    k, sk, val = tr[pos[e]]
                    if k == "w":
                        if sem.get(sk, 0) < val:
                            break
                    else:
                        sem[sk] = sem.get(sk, 0) + val
                    pos[e] += 1
                    progress = True
        stuck = {e: (pos[e], len(self.tr[e]), self.tr[e][pos[e]], sem.get(self.tr[e][pos[e]][1], 0))
                 for e in ENGS if pos[e] < len(self.tr[e])}
        assert not stuck, "DEADLOCK: %r" % (stuck,)

    def wait_all(self, eng, views):
        self._emit_waits(eng, self._deps(eng, views, []))

    def mm(self, out, lhsT, rhs, start, stop, signal=None, skip=False):
        if signal is None:
            signal = stop
        o, l, r = out.ap, lhsT.ap, rhs.ap
        if skip:
            fn = lambda e: e.matmul(o, l, r, start=start, stop=stop, skip_group_check=True)
        else:
            fn = lambda e: e.matmul(o, l, r, start=start, stop=stop)
        return self.op("pe", fn, reads=[lhsT, rhs], writes=[out], signal=signal)

    def act(self, out, in_, func, bias=None, scale=None, extra_reads=()):
        o, i = out.ap, in_.ap
        kw = {}
        if bias is not None:
            kw["bias"] = bias
        if scale is not None:
            kw["scale"] = scale
        return self.op("act", lambda e: e.activation(o, i, func, **kw),
                       reads=[in_] + list(extra_reads), writes=[out])

    def tt(self, eng, out, a, b, op):
        o, x, y = out.ap, a.ap, b.ap
        return self.op(eng, lambda e: e.tensor_tensor(o, x, y, op), reads=[a, b], writes=[out])

    def ts(self, eng, out, a, s1, s2, op0, op1=None, extra_reads=()):
        o, x = out.ap, a.ap
        if op1 is None:
            fn = lambda e: e.tensor_scalar(o, x, s1, s2, op0)
        else:
            fn = lambda e: e.tensor_scalar(o, x, s1, s2, op0, op1)
        return self.op(eng, fn, reads=[a] + list(extra_reads), writes=[out])

    def stt(self, eng, out, a, s, b, op0, op1, extra_reads=()):
        o, x, y = out.ap, a.ap, b.ap
        return self.op(eng, lambda e: e.scalar_tensor_tensor(o, x, s, y, op0, op1),
                       reads=[a, b] + list(extra_reads), writes=[out])

    def copy(self, eng, out, in_):
        o, i = out.ap, in_.ap
        return self.op(eng, (lambda e: e.copy(o, i)) if eng == "act" else (lambda e: e.tensor_copy(o, i)), reads=[in_], writes=[out])

    def recip(self, eng, out, in_):
        o, i = out.ap, in_.ap
        return self.op(eng, lambda e: e.reciprocal(o, i), reads=[in_], writes=[out])

    def memset(self, eng, out, val):
        o = out.ap
        return self.op(eng, lambda e: e.memset(o, val), writes=[out])


def piece_shapes():
    ffn_p = []
    f0 = 0
    while f0 < 44:
        nf = min(8, 44 - f0)
        ffn_p += [(D, 2 * nf * 128), (nf * 128, D)]
        f0 += nf
    return ffn_p + [(D, 2048)] * 3 + [(1024, D)] * 2 + ffn_p


def make_pieces(w1_gu, w1_down, w_in, w_out, w2_gu, w2_down):
    def ffn_p(wgu, wdn):
        out = []
        f0 = 0
        while f0 < 44:
            nf = min(8, 44 - f0)
            a, b = f0 * 128, (f0 + nf) * 128
            out.append(np.concatenate([wgu[:, a:b], wgu[:, DFF + a:DFF + b]], axis=1))
            out.append(wdn[a:b, :])
            f0 += nf
        return out
    cols = []
    for blk in range(4):
        for base in (3072, 4096, 5120):
            cols.append(np.arange(base + blk * 256, base + blk * 256 + 256))
    cols += [np.arange(0, 1024), np.arange(1024, 2048), np.arange(2048, 3072)]
    winp = w_in[:, np.concatenate(cols)]
    ps = ffn_p(w1_gu, w1_down) + [winp[:, 0:2048], winp[:, 2048:4096], winp[:, 4096:6144]]
    ps += [w_out[1024:2048, :], w_out[0:1024, :]] + ffn_p(w2_gu, w2_down)
    return ps


def build_program():
    nc = bass.Bass("TRN2", target_bir_lowering=False)
    es = ExitStack()
    with es:
        es.enter_context(nc.allow_low_precision("bf16 matmul operands, fp32 accumulation"))
        kb = KB(nc, es)

        def din(name, shape, dt=F32):
            return nc.dram_tensor(name, shape, dt, kind="ExternalInput").ap()

        xT_d = din("xT", [D, NT])
        cT_d = din("cT", [D, 2])
        wada_d = din("w_ada", [D, 2304])
        badaT_d = din("b_adaT", [128, 18])
        qg_d = din("q_g", [128, 1])
        kg_d = din("k_g", [128, 1])
        cw_d = din("conv_wT", [128, 24])
        consts_d = din("consts", [128, 384], BF16)
        outT_d = nc.dram_tensor("outT", [D, NT], F32, kind="ExternalOutput").ap()

        def dint(name, shape, dt):
            return nc.dram_tensor(name, shape, dt, kind="Internal").ap()

        mod_in = dint("mod_in", [128, 36], F32)
        mod_all = dint("mod_all", [1024, 36], F32)
        halo_in = dint("halo_in", [1024, 2], F32)
        halo_all = dint("halo_all", [4096, 2], F32)
        hbuf = dint("hbuf", [5120, 2], F32)
        q_in = dint("q_in", [1024, NT], BF16)
        q_all = dint("q_all", [8192, NT], BF16)
        k_in = dint("k_in", [1024, NT], BF16)
        k_all = dint("k_all", [8192, NT], BF16)
        v_in = dint("v_in", [1024, 1024], BF16)
        v_all = dint("v_all", [8192, 1024], BF16)
        att_in = dint("att_in", [256, SEQ], BF16)
        att_all = dint("att_all", [2048, SEQ], BF16)

        def dreg(name, ap, ncell=1):
            return Reg(name, ap, ncell, 1)

        def sb(name, n, dt):
            return es.enter_context(nc.sbuf_tensor(name, [128, n], dt))

        X = Reg("X", sb("X", NFT * NT, F32), NFT * NT, 512)
        H = Reg("H", sb("H", 16384, BF16), 16384, 512)
        A = Reg("A", sb("A", 8192, BF16), 8192, 512)
        TF = Reg("TF", sb("TF", 5120, F32), 5120, 512)
        TB = Reg("TB", sb("TB", 5120, BF16), 5120, 512)
        W = [Reg("W%d" % i, sb("W%d" % i, 4096, BF16), 4096, 4096) for i in range(NSLOT)]
        MODT = Reg("MODT", sb("MODT", 144, F32), 144, 144)
        MD = Reg("MD", sb("MD", 96, F32), 96, 16)
        CON = Reg("CON", sb("CON", 384, BF16), 384, 384)
        SM = Reg("SM", sb("SM", 256, F32), 256, 8)
        CB = Reg("CB", sb("CB", 32, BF16), 32, 32)
        PS = [Reg("PS%d" % i, es.enter_context(nc.psum_tensor("PS%d" % i, [128, 512], F32)), 512, 128)
              for i in range(8)]

        ones = CON.v(0, 128)
        tri1 = CON.v(128, 256)
        tri2 = CON.v(256, 384)
        cT_sb = SM.v(0, 32)
        modout = SM.v(32, 68)
        badaT = SM.v(68, 86)
        qg = SM.v(86, 87)
        kg = SM.v(87, 88)
        cw = SM.v(88, 112)
        halo_out = SM.v(112, 128)
        cu01 = SM.v(128, 144)
        gb01 = SM.v(144, 160)
        halo_sb = SM.v(160, 176)
        y01 = SM.v(176, 192)
        tmp16 = SM.v(192, 208)
        zeros16 = SM.v(216, 232)

        R_modin = dreg("mod_in", mod_in)
        R_modall = dreg("mod_all", mod_all)
        R_haloin = dreg("halo_in", halo_in)
        R_haloall = dreg("halo_all", halo_all)
        R_hbuf = dreg("hbuf", hbuf, 2)
        R_qin = dreg("q_in", q_in, 16)
        R_qall = dreg("q_all", q_all)
        R_kin = dreg("k_in", k_in, 16)
        R_kall = dreg("k_all", k_all)
        R_vin = dreg("v_in", v_in, 32)
        R_vall = dreg("v_all", v_all)
        R_attin = dreg("att_in", att_in, 16)
        R_attall = dreg("att_all", att_all)
        R_out = dreg("outT", outT_d)

        def dv(reg, ap, lo=0, hi=None):
            hi = reg.n if hi is None else hi
            return V(ap, reg.keys(lo, hi))

        pid_cache = {}

        def pid(e):
            k = id(e)
            if k not in pid_cache:
                pid_cache[k] = e.partition_id()
            return pid_cache[k]

        wstate = {"i": 0}

        pieces = []
        for (pk, pn) in piece_shapes():
            i = len(pieces)
            ext = din("wp%d" % i, [pk // 8, pn])
            loc = dint("wl%d" % i, [pk // 8, pn], F32)
            full = dint("wf%d" % i, [pk, pn], F32)
            pieces.append(dict(ext=ext, loc=loc, full=full, Rl=dreg("wl%d" % i, loc), Rf=dreg("wf%d" % i, full)))
        pstate = {"n": 0}
        LOOK = 3

        def ensure(idx):
            while pstate["n"] < min(idx + 1 + LOOK, len(pieces)):
                p = pieces[pstate["n"]]
                pstate["n"] += 1
                kb.dma("sp", dv(p["Rl"], p["loc"]), V(p["ext"], []))
                kb.cc(dv(p["Rl"], p["loc"]), dv(p["Rf"], p["full"]), G8)

        def wload(src, kt, ncol):
            s = W[wstate["i"] % NSLOT]
            wstate["i"] += 1
            dst = s.v(0, kt * ncol, "p (t n) -> p t n", t=kt)
            if isinstance(src, tuple):
                pi, r0, r1, c0, c1 = src
                ensure(pi)
                p = pieces[pi]
                sv = dv(p["Rf"], p["full"][r0:r1, c0:c1].rearrange("(t p) n -> p t n", p=128))
            else:
                sv = V(src.rearrange("(t p) n -> p t n", p=128), [])
            kb.dma("pool", dst, sv)
            return s, dst

        def wslice(s, kt_i, ncol, c0, c1):
            return s.v(kt_i * ncol + c0, kt_i * ncol + c1)

        kb.dma("sp", CON.v(0, 384), V(consts_d, []))
        kb.dma("sp", SM.v(0, 32, "p (kt b) -> p kt b", b=2), V(cT_d.rearrange("(kt p) b -> p kt b", p=128), []))
        kb.dma("sp", badaT, V(badaT_d, []))
        kb.dma("sp", qg, V(qg_d, []))
        kb.dma("sp", kg, V(kg_d, []))
        kb.dma("sp", cw, V(cw_d, []))
        for ft in range(NFT):
            kb.dma("sp", X.v(ft * NT, (ft + 1) * NT), V(xT_d[ft * 128:(ft + 1) * 128, :], []))
        kb.memset("dve", SM.v(208, 209), EPS)
        kb.memset("dve", zeros16, 0.0)
        kb.dma("sp", dv(R_hbuf, hbuf[0:1024, :].rearrange("(ct p) k -> p ct k", p=128), 0, 1),
               SM.v(216, 232, "p (ct k) -> p ct k", k=2))
        eps_ap = SM.t[:, 208:209]

        kb.act(CB.v(0, 32), cT_sb, AF.Silu)
        mps = PS[4]
        for blk in range(9):
            s, _ = wload(wada_d[:, blk * 256:(blk + 1) * 256], 16, 256)
            for h in range(2):
                t = blk * 2 + h
                for kt in range(16):
                    kb.mm(mps.v(2 * t, 2 * t + 2), wslice(s, kt, 256, h * 128, h * 128 + 128),
                          CB.v(2 * kt, 2 * kt + 2), start=(kt == 0), stop=(kt == 15))
        for b in range(2):
            src = V(mps.t[:, 0:36].rearrange("p (t b) -> p t b", b=2)[:, :, b], mps.keys(0, 36))
            kb.tt("dve", SM.v(32 + 18 * b, 50 + 18 * b), src, badaT, ALU.add)
        kb.dma("sp", dv(R_modin, mod_in), modout)
        kb.cc(dv(R_modin, mod_in), dv(R_modall, mod_all), G8)

        def mod_src(e):
            b = pid(e) // 4
            return mod_all[:, bass.ds(b * 18, 18)].rearrange("(r p) t -> p r t", p=128)

        kb.dma("sp", MODT.v(0, 144, "p (r t) -> p r t", r=8), dv(R_modall, mod_src))
        for i, m in enumerate((1, 4, 7)):
            kb.ts("dve", MD.v(16 * i, 16 * i + 16), MODT.v(16 * m, 16 * m + 16), 1.0, None, ALU.add)
        kb.ts("dve", MD.v(48, 64), MODT.v(32, 48), 0.5, None, ALU.mult)
        kb.ts("dve", MD.v(64, 80), MODT.v(80, 96), 1.0, None, ALU.mult)
        kb.ts("dve", MD.v(80, 96), MODT.v(128, 144), 0.5, None, ALU.mult)

        def xv(ft, c0=0, c1=NT):
            return X.v(ft * NT + c0, ft * NT + c1)

        def hv(ft, c0=0, c1=NT):
            return H.v(ft * NT + c0, ft * NT + c1)

        rstd = TF.v(0, 1024)

        def norm_mod(shcol, sccol):
            for ft in range(NFT):
                sq = TB.v((ft % 2) * 1024, (ft % 2) * 1024 + 1024)
                kb.act(sq, xv(ft), AF.Square)
                for c in range(2):
                    kb.mm(PS[c].v(0, 512), ones, TB.v((ft % 2) * 1024 + c * 512, (ft % 2) * 1024 + c * 512 + 512),
                          start=(ft == 0), stop=(ft == NFT - 1), signal=True)
            for c in range(2):
                r = TF.v(c * 512, c * 512 + 512)
                kb.act(r, PS[c].v(0, 512), AF.Sqrt, bias=eps_ap, scale=1.0 / D, extra_reads=[SM.v(208, 209)])
                kb.recip("dve", r, r)
            for ft in range(NFT):
                tmpx = TF.v(1024 + (ft % 2) * 1024, 2048 + (ft % 2) * 1024)
                kb.tt("dve", tmpx, xv(ft), rstd, ALU.mult)
                kb.act(hv(ft), tmpx, AF.Identity,
                       bias=MODT.t[:, shcol + ft:shcol + ft + 1], scale=MD.t[:, sccol + ft:sccol + ft + 1],
                       extra_reads=[MODT.v(shcol, shcol + 16), MD.v(sccol, sccol + 16)])

        def ffn(pbase, gcol):
            gu_i = 0
            dn_i = 0
            f0 = 0
            sp_i = -1
            while f0 < 44:
                nf = min(8, 44 - f0)
                sp_i += 1
                wc = nf * 128
                for pr in range(nf // 2):
                    fb = f0 + pr * 2
                    sg, _ = wload((pbase + 2 * sp_i, 0, D, pr * 256, pr * 256 + 256), 16, 256)
                    su, _ = wload((pbase + 2 * sp_i, 0, D, wc + pr * 256, wc + pr * 256 + 256), 16, 256)
                    for h in range(2):
                        fl = pr * 2 + h
                        for c in range(2):
                            bset = gu_i % 2
                            gu_i += 1
                            pg, pu = PS[2 * bset], PS[2 * bset + 1]
                            for kt in range(16):
                                kb.mm(pg.v(0, 512), wslice(sg, kt, 256, h * 128, h * 128 + 128),
                                      hv(kt, c * 512, c * 512 + 512), start=(kt == 0), stop=(kt == 15))
                            for kt in range(16):
                                kb.mm(pu.v(0, 512), wslice(su, kt, 256, h * 128, h * 128 + 128),
                                      hv(kt, c * 512, c * 512 + 512), start=(kt == 0), stop=(kt == 15))
                            stmp = TF.v(3072 + bset * 512, 3584 + bset * 512)
                            kb.act(stmp, pg.v(0, 512), AF.Silu)
                            kb.tt("dve", A.v(fl * NT + c * 512, fl * NT + c * 512 + 512), stmp, pu.v(0, 512), ALU.mult)
                for dg in range(4):
                    sd, _ = wload((pbase + 2 * sp_i + 1, 0, nf * 128, dg * 512, (dg + 1) * 512), nf, 512)
                    for dl in range(4):
                        dt_ = dg * 4 + dl
                        bset = dn_i % 2
                        dn_i += 1
                        for c in range(2):
                            pd = PS[4 + 2 * bset + c]
                            for fl in range(nf):
                                kb.mm(pd.v(0, 512), wslice(sd, fl, 512, dl * 128, dl * 128 + 128),
                                      A.v(fl * NT + c * 512, fl * NT + c * 512 + 512),
                                      start=(fl == 0), stop=(fl == nf - 1))
                            xs = xv(dt_, c * 512, c * 512 + 512)
                            kb.stt("dve", xs, pd.v(0, 512), MD.t[:, gcol + dt_:gcol + dt_ + 1], xs,
                                   ALU.mult, ALU.add, extra_reads=[MD.v(gcol, gcol + 16)])
                f0 += nf

        if STAGE >= 1 and DO_FFN1:
            norm_mod(0, 0)
            ffn(0, 48)

        if STAGE >= 2:
            mixer(kb, locals())
        if STAGE >= 3:
            g_ = locals()
            norm_mod(96, 32)
            ffn(17, 80)

        ev = kb.dma("sp", dv(R_out, outT_d.rearrange("(ft p) t -> p ft t", p=128)),
                    X.v(0, NFT * NT, "p (ft t) -> p ft t", ft=NFT))
        kb.wait_all("sp", [dv(R_out, None)])
        for e in ENGS:
            assert not kb.pending[e], e
        kb.check_deadlock()

        with nc.Block() as block:
            @block.tensor
            def _(e):
                for f in kb.q["pe"]:
                    f(e)

            @block.scalar
            def _(e):
                for f in kb.q["act"]:
                    f(e)

            @block.vector
            def _(e):
                for f in kb.q["dve"]:
                    f(e)

            @block.gpsimd
            def _(e):
                for f in kb.q["pool"]:
                    f(e)

            @block.sync
            def _(e):
                for f in kb.q["sp"]:
                    f(e)
    return nc


def mixer(kb, L):
    g = dict(L)
    X, H, A, TF, TB, PS, SM, MD, MODT = (g[k] for k in ("X", "H", "A", "TF", "TB", "PS", "SM", "MD", "MODT"))
    ones, tri1, tri2 = g["ones"], g["tri1"], g["tri2"]
    wload, wslice, xv, hv, dv, pid = g["wload"], g["wslice"], g["xv"], g["hv"], g["dv"], g["pid"]
    winblk = lambda bi: (12 + bi // 8, 0, D, (bi % 8) * 256, (bi % 8) * 256 + 256)
    qg, kg, cw = g["qg"], g["kg"], g["cw"]
    halo_out, cu01, gb01, halo_sb, y01, tmp16 = (g[k] for k in ("halo_out", "cu01", "gb01", "halo_sb", "y01", "tmp16"))

    g["norm_mod"](48, 16)

    cu = TF.v(1024, 2048)
    yv = TF.v(2048, 3072)
    gb = TF.v(3072, 4096)
    it = 0
    for blk in range(4):
        sb_, _ = wload(winblk(blk * 3), 16, 256)
        sc_, _ = wload(winblk(blk * 3 + 1), 16, 256)
        su_, _ = wload(winblk(blk * 3 + 2), 16, 256)
        for h in range(2):
            ct = blk * 2 + h
            for c in range(2):
                bset = it % 2
                it += 1
                pb, pc, pu = PS[3 * bset], PS[3 * bset + 1], PS[3 * bset + 2]
                for (pp, ss) in ((pb, sb_), (pc, sc_), (pu, su_)):
                    for kt in range(16):
                        kb.mm(pp.v(0, 512), wslice(ss, kt, 256, h * 128, h * 128 + 128),
                              hv(kt, c * 512, c * 512 + 512), start=(kt == 0), stop=(kt == 15))
                utmp = TF.v(4096 + bset * 512, 4608 + bset * 512)
                kb.copy("act", utmp, pu.v(0, 512))
                kb.tt("dve", TF.v(1024 + c * 512, 1536 + c * 512), pc.v(0, 512), utmp, ALU.mult)
                kb.copy("act", TF.v(3072 + c * 512, 3584 + c * 512), pb.v(0, 512))
            cwv = lambda k: SM.t[:, 88 + ct * 3 + k:88 + ct * 3 + k + 1]
            yo = TF.v(2048 + 2, 3072)
            kb.ts("dve", yo, TF.v(1024, 2046), cwv(0), None, ALU.mult, extra_reads=[cw])
            kb.stt("dve", yo, TF.v(1025, 2047), cwv(1), yo, ALU.mult, ALU.add, extra_reads=[cw])
            kb.stt("dve", yo, TF.v(1026, 2048), cwv(2), yo, ALU.mult, ALU.add, extra_reads=[cw])
            kb.tt("dve", A.v(ct * NT + 2, ct * NT + NT), TF.v(3072 + 2, 4096), yo, ALU.mult)
            kb.copy("dve", SM.v(112 + 2 * ct, 114 + 2 * ct), TF.v(2046, 2048))
            kb.copy("dve", SM.v(128 + 2 * ct, 130 + 2 * ct), TF.v(1024, 1026))
            kb.copy("dve", SM.v(144 + 2 * ct, 146 + 2 * ct), TF.v(3072, 3074))

    halo_in, halo_all, hbuf = g["halo_in"], g["halo_all"], g["hbuf"]
    R_haloin, R_haloall, R_hbuf = g["R_haloin"], g["R_haloall"], g["R_hbuf"]
    kb.dma("sp", dv(R_haloin, halo_in.rearrange("(ct p) k -> p ct k", p=128)),
           SM.v(112, 128, "p (ct k) -> p ct k", k=2))
    kb.cc(dv(R_haloin, halo_in), dv(R_haloall, halo_all), G4)
    kb.dma("sp", dv(R_hbuf, hbuf[1024:5120, :], 1, 2), dv(R_haloall, halo_all))

    def halo_src(e):
        j = pid(e) % 4
        return hbuf[bass.ds(j * 1024, 1024), :].rearrange("(ct p) k -> p ct k", p=128)

    kb.dma("sp", SM.v(160, 176, "p (ct k) -> p ct k", k=2), dv(R_hbuf, halo_src, 0, 2))

    if MIX_STOP == 1:
        return
    q_in, k_in, v_in = g["q_in"], g["k_in"], g["v_in"]
    R_qin, R_kin, R_vin = g["R_qin"], g["R_kin"], g["R_vin"]
    it = 0
    for (col0, dst, Rdst, gain) in ((0, q_in, R_qin, 86), (1024, k_in, R_kin, 87)):
        for blk in range(4):
            s, _ = wload(winblk((12 if col0 == 0 else 16) + blk), 16, 256)
            for h in range(2):
                hd = blk * 2 + h
                for c in range(2):
                    bset = it % 2
                    it += 1
                    pq, pss = PS[6 + bset], PS[4 + bset]
                    for kt in range(16):
                        kb.mm(pq.v(0, 512), wslice(s, kt, 256, h * 128, h * 128 + 128),
                              hv(kt, c * 512, c * 512 + 512), start=(kt == 0), stop=(kt == 15))
                    sq = TB.v(bset * 512, bset * 512 + 512)
                    kb.act(sq, pq.v(0, 512), AF.Square)
                    kb.mm(pss.v(0, 512), ones, sq, start=True, stop=True)
                    r = TF.v(bset * 512, bset * 512 + 512)
                    kb.act(r, pss.v(0, 512), AF.Sqrt, bias=SM.t[:, 208:209], scale=1.0 / 128.0, extra_reads=[SM.v(208, 209)])
                    kb.recip("dve", r, r)
                    st = TB.v(1024 + (it % 4) * 512, 1536 + (it % 4) * 512)
                    kb.stt("dve", st, pq.v(0, 512), SM.t[:, gain:gain + 1], r, ALU.mult, ALU.mult,
                           extra_reads=[SM.v(gain, gain + 1)])
                    kb.dma("sp", dv(Rdst, dst[hd * 128:(hd + 1) * 128, c * 512:(c + 1) * 512], hd * 2 + c, hd * 2 + c + 1), st)
        if col0 == 0:
            kb.cc(dv(R_qin, q_in), dv(g["R_qall"], g["q_all"]), G8)
        else:
            kb.cc(dv(R_kin, k_in), dv(g["R_kall"], g["k_all"]), G8)
    if MIX_STOP == 11:
        return
    it = 0
    for blk in range(4):
        s, _ = wload(winblk(20 + blk), 16, 256)
        for tt in range(8):
            bset = it % 2
            it += 1
            pv = PS[6 + bset]
            for kt in range(16):
                kb.mm(pv.v(0, 256), hv(kt, tt * 128, tt * 128 + 128), wslice(s, kt, 256, 0, 256),
                      start=(kt == 0), stop=(kt == 15))
            st = TB.v(3072 + bset * 256, 3328 + bset * 256)
            kb.copy("act", st, pv.v(0, 256))
            dst = v_in[tt * 128:(tt + 1) * 128, blk * 256:(blk + 1) * 256]
            kb.dma("sp", dv(R_vin, dst, blk * 8 + tt, blk * 8 + tt + 1), st)
    kb.cc(dv(R_vin, v_in), dv(g["R_vall"], g["v_all"]), G8)

    if MIX_STOP == 12:
        return
    S = lambda a, b, **kw: SM.v(a, b, **kw)

    def s3(lo, k):
        return V(SM.t[:, lo:lo + 16].rearrange("p (ct k) -> p ct k", k=2)[:, :, k], SM.keys(lo, lo + 16))

    def cw3(k):
        return V(SM.t[:, 88:112].rearrange("p (ct k) -> p ct k", k=3)[:, :, k], SM.keys(88, 112))

    h0, h1, c0_, c1_ = s3(160, 0), s3(160, 1), s3(128, 0), s3(128, 1)
    y0, y1, t0 = s3(176, 0), s3(176, 1), SM.v(192, 200)
    kb.tt("dve", y0, h0, cw3(0), ALU.mult)
    kb.tt("dve", t0, h1, cw3(1), ALU.mult)
    kb.tt("dve", y0, y0, t0, ALU.add)
    kb.tt("dve", t0, c0_, cw3(2), ALU.mult)
    kb.tt("dve", y0, y0, t0, ALU.add)
    kb.tt("dve", y1, h1, cw3(0), ALU.mult)
    kb.tt("dve", t0, c0_, cw3(1), ALU.mult)
    kb.tt("dve", y1, y1, t0, ALU.add)
    kb.tt("dve", t0, c1_, cw3(2), ALU.mult)
    kb.tt("dve", y1, y1, t0, ALU.add)
    conv01 = V(A.t[:, 0:8192].rearrange("p (ct t) -> p ct t", ct=8)[:, :, 0:2], A.keys(0, 8192))
    kb.tt("dve", conv01, SM.v(176, 192, "p (ct k) -> p ct k", k=2), SM.v(144, 160, "p (ct k) -> p ct k", k=2), ALU.mult)

    if MIX_STOP == 13:
        return
    def wout_half(row0, src_fn):
        it2 = 0
        for dg in range(4):
            s, _ = wload((15 if row0 == 1024 else 16, 0, 1024, dg * 512, (dg + 1) * 512), 8, 512)
            for dl in range(4):
                dt_ = dg * 4 + dl
                bset = it2 % 2
                it2 += 1
                for c in range(2):
                    pd = PS[4 + 2 * bset + c]
                    for kt in range(8):
                        kb.mm(pd.v(0, 512), wslice(s, kt, 512, dl * 128, dl * 128 + 128),
                              src_fn(kt, c), start=(kt == 0), stop=(kt == 7))
                    xs = xv(dt_, c * 512, c * 512 + 512)
                    kb.stt("dve", xs, pd.v(0, 512), MD.t[:, 64 + dt_:64 + dt_ + 1], xs,
                           ALU.mult, ALU.add, extra_reads=[MD.v(64, 80)])

    wout_half(1024, lambda kt, c: A.v(kt * NT + c * 512, kt * NT + c * 512 + 512))

    if MIX_STOP == 2:
        return
    q_all, k_all, v_all = g["q_all"], g["k_all"], g["v_all"]
    R_qall, R_kall, R_vall = g["R_qall"], g["R_kall"], g["R_vall"]
    for e_ in range(2):
        def qsrc(e, e_=e_):
            return q_all.rearrange("(g r hd d) t -> g hd d r t", g=2, r=4, hd=8)[
                bass.ds(pid(e) // 4, 1), bass.ds((pid(e) % 4) * 2 + e_, 1)].rearrange("a o d r t -> (a o d) r t")

        def ksrc(e, e_=e_):
            return k_all.rearrange("(g r hd d) t -> g hd d r t", g=2, r=4, hd=8)[
                bass.ds(pid(e) // 4, 1), bass.ds((pid(e) % 4) * 2 + e_, 1)].rearrange("a o d r t -> (a o d) r t")

        def vsrc(e, e_=e_):
            return v_all[bass.ds((pid(e) // 4) * 4096, 4096), bass.ds((pid(e) % 4) * 256 + e_ * 128, 128)].rearrange(
                "(kb p) d -> p kb d", p=128)

        kb.dma("sp", H.v(e_ * 4096, e_ * 4096 + 4096, "d (r t) -> d r t", r=4), dv(R_qall, qsrc))
        kb.dma("sp", H.v(8192 + e_ * 4096, 8192 + e_ * 4096 + 4096, "d (r t) -> d r t", r=4), dv(R_kall, ksrc))
        kb.dma("sp", A.v(e_ * 4096, e_ * 4096 + 4096, "p (kb d) -> p kb d", d=128), dv(R_vall, vsrc))

    if MIX_STOP == 3:
        return
    att_in, att_all = g["att_in"], g["att_all"]
    R_attin, R_attall = g["R_attin"], g["R_attall"]
    SCALE = 1.0 / float(np.sqrt(128.0))
    for qc in range(8):
        nkb = 4 * qc + 4
        for step in range(nkb):
            kbk = nkb - 1 - step
            i = kbk - 4 * qc
            c0 = 128 * i if i > 0 else 0
            first = (step == 0)
            last = (step == nkb - 1)
            for ch in range(2):
                Z, Rb, O = PS[ch], PS[2 + ch], PS[4 + ch]
                par = (step % 2) * 2 + ch
                e_sb = TF.v(par * 512, par * 512 + 512)
                w_sb = TF.v(2048 + par * 512, 2560 + par * 512)
                L_sb = TB.v(par * 512, par * 512 + 512)
                A_sb = TB.v(2048 + par * 512, 2560 + par * 512)
                sl = lambda reg_v, base: reg_v(base + c0, base + 512)
                kT = H.v(8192 + ch * 4096 + kbk * 128, 8192 + ch * 4096 + kbk * 128 + 128)
                qT = H.v(ch * 4096 + qc * 512 + c0, ch * 4096 + qc * 512 + 512)
                kb.mm(Z.v(c0, 512), kT, qT, start=True, stop=True)
                kb.act(TF.v(par * 512 + c0, par * 512 + 512), Z.v(c0, 512), AF.Exp, scale=SCALE)
                kb.act(TB.v(par * 512 + c0, par * 512 + 512), TF.v(par * 512 + c0, par * 512 + 512), AF.Ln, bias=1.0)
                if i >= 0:
                    dL = TB.v(par * 512 + c0, par * 512 + c0 + 128)
                    kb.tt("dve", dL, dL, tri2, ALU.mult)
                Lv = TB.v(par * 512 + c0, par * 512 + 512)
                kb.mm(Rb.v(c0, 512), tri1, Lv, start=first, stop=False, signal=True, skip=True)
                kb.act(TF.v(2048 + par * 512 + c0, 2560 + par * 512), Rb.v(c0, 512), AF.Exp, scale=-1.0)
                if not last:
                    kb.mm(Rb.v(c0, 512), tri2, Lv, start=False, stop=False, signal=True, skip=True)
                Av = TB.v(2048 + par * 512 + c0, 2560 + par * 512)
                kb.tt("dve", Av, TF.v(par * 512 + c0, par * 512 + 512), TF.v(2048 + par * 512 + c0, 2560 + par * 512), ALU.mult)
                if i >= 0:
                    dA = TB.v(2048 + par * 512 + c0, 2048 + par * 512 + c0 + 128)
                    kb.tt("dve", dA, dA, tri2, ALU.mult)
                vS = A.v(ch * 4096 + kbk * 128, ch * 4096 + kbk * 128 + 128)
                kb.mm(O.v(c0, 512), vS, Av, start=first, stop=last, signal=True, skip=True)
        for ch in range(2):
            ost = TB.v(4096 + ch * 512, 4608 + ch * 512)
            kb.copy("act", ost, PS[4 + ch].v(0, 512))
            kb.dma("sp", dv(R_attin, att_in[ch * 128:(ch + 1) * 128, qc * 512:(qc + 1) * 512], qc * 2 + ch, qc * 2 + ch + 1), ost)
    kb.cc(dv(R_attin, att_in), dv(R_attall, att_all), G8)

    def att_src(e):
        return att_all[bass.ds((pid(e) // 4) * 1024, 1024), bass.ds((pid(e) % 4) * 1024, 1024)].rearrange(
            "(ft p) t -> p ft t", p=128)

    kb.dma("sp", H.v(0, 8192, "p (ft t) -> p ft t", ft=8), dv(R_attall, att_src))
    wout_half(0, lambda kt, c: H.v(kt * NT + c * 512, kt * NT + c * 512 + 512))


_NC_CACHE = {}


def _get_nc():
    if "nc" not in _NC_CACHE:
        _NC_CACHE["nc"] = build_program()
    return _NC_CACHE["nc"]


def kernel(x, c, w_ada, b_ada, w1_gu, w1_down, w_in, q_norm_w, k_norm_w, conv_w, w_out, w2_gu, w2_down):
    f32 = np.float32
    x = np.asarray(x, f32)
    c = np.asarray(c, f32)
    w_ada = np.asarray(w_ada, f32)[0]
    b_ada = np.asarray(b_ada, f32)[0]
    f32a = lambda a: np.asarray(a, f32)[0]
    pcs = make_pieces(f32a(w1_gu), f32a(w1_down), f32a(w_in), f32a(w_out), f32a(w2_gu), f32a(w2_down))
    for p_, (pk, pn) in zip(pcs, piece_shapes()):
        assert p_.shape == (pk, pn), (p_.shape, pk, pn)
    shared = {
        "cT": np.ascontiguousarray(c.T),
        "q_g": np.ascontiguousarray(np.asarray(q_norm_w, f32)[0].reshape(128, 1)),
        "k_g": np.ascontiguousarray(np.asarray(k_norm_w, f32)[0].reshape(128, 1)),
        "conv_wT": np.ascontiguousarray(
            np.asarray(conv_w, f32)[0].reshape(3, 8, 128).transpose(2, 1, 0).reshape(128, 24)),
    }
    j = np.arange(128)[:, None]
    s = np.arange(128)[None, :]
    consts = np.concatenate([np.ones((128, 128)), (j >= s), (j < s)], axis=1).astype(ml_dtypes.bfloat16)
    shared["consts"] = consts
    in_maps = []
    for i in range(8):
        b, jr = i // 4, i % 4
        m = dict(shared)
        m["xT"] = np.ascontiguousarray(x[b, jr * NT:(jr + 1) * NT, :].T)
        m["w_ada"] = np.ascontiguousarray(w_ada[:, i * 2304:(i + 1) * 2304])
        m["b_adaT"] = np.ascontiguousarray(b_ada[i * 2304:(i + 1) * 2304].reshape(18, 128).T)
        for pi, p_ in enumerate(pcs):
            kk = p_.shape[0] // 8
            m["wp%d" % pi] = np.ascontiguousarray(p_[i * kk:(i + 1) * kk, :])
        in_maps.append(m)
    nc = _get_nc()
    res = run_bass_kernel_spmd(nc, in_maps, core_ids=list(range(8)))
    out = np.empty((2, SEQ, D), f32)
    for i in range(8):
        b, jr = i // 4, i % 4
        out[b, jr * NT:(jr + 1) * NT, :] = res.results[i]["outT"].T
    return out
```

```python
import numpy as np
import ml_dtypes
from contextlib import ExitStack

import concourse.bass as bass
import concourse.mybir as mybir
from concourse.bass_utils import run_bass_kernel_spmd

F32 = mybir.dt.float32
BF16 = mybir.dt.bfloat16
AF = mybir.ActivationFunctionType
ALU = mybir.AluOpType

D = 2048
DFF = 5632
NT = 1024
SEQ = 4096
NFT = 16
EPS = 1e-6
G4 = [[0, 1, 2, 3], [4, 5, 6, 7]]
G8 = [list(range(8))]
ENGS = ["pe", "act", "dve", "pool", "sp"]
NSLOT = 7
STAGE = 99
DO_FFN1 = True
MIX_STOP = 99


class V:
    def __init__(self, ap, keys):
        self.ap = ap
        self.keys = keys


class Reg:
    def __init__(self, name, t, n, cell):
        self.name, self.t, self.n, self.cell = name, t, n, cell

    def keys(self, lo, hi):
        return [(self.name, c) for c in range(lo // self.cell, (hi - 1) // self.cell + 1)]

    def v(self, lo, hi, pat=None, p0=0, p1=128, **kw):
        ap = self.t[p0:p1, lo:hi]
        if pat is not None:
            ap = ap.rearrange(pat, **kw)
        return V(ap, self.keys(lo, hi))


class KB:
    def __init__(self, nc, es):
        self.nc, self.es = nc, es
        self.q = {e: [] for e in ENGS}
        self.cnt = {e: 0 for e in ENGS}
        self.pending = {e: False for e in ENGS}
        self.semh = {}
        self.waited = {e: {} for e in ENGS}
        self.cells = {}
        self.tr = {e: [] for e in ENGS}
        self.dma_i = {"sp": 0, "pool": 0, "act": 0}
        self.NDS = 8
        self.ncc = 0
        for e in ENGS:
            self.semh[e] = es.enter_context(nc.semaphore("s_" + e))
        for e in ("sp", "pool"):
            for i in range(self.NDS):
                self.semh[("d", e, i)] = es.enter_context(nc.semaphore("d_%s_%d" % (e, i)))

    def _deps(self, eng, reads, writes):
        deps = {}

        def need(ev):
            if ev is None:
                return
            sk, val = ev
            if eng == "pe" and sk == "pe":
                return
            if deps.get(sk, 0) < val:
                deps[sk] = val

        for v in reads:
            for k in v.keys:
                c = self.cells.get(k)
                if c:
                    need(c[0])
        for v in writes:
            for k in v.keys:
                c = self.cells.get(k)
                if c:
                    need(c[0])
                    for sk, val in c[1].items():
                        need((sk, val))
        return deps

    def _emit_waits(self, eng, deps):
        for sk, val in deps.items():
            if self.waited[eng].get(sk, 0) >= val:
                continue
            self.waited[eng][sk] = val
            sem = self.semh[sk]
            self.tr[eng].append(("w", sk, val))
            self.q[eng].append(lambda e, sem=sem, val=val: e.wait_ge(sem, val))

    def _record(self, ev, reads, writes):
        for v in writes:
            for k in v.keys:
                self.cells[k] = [ev, {}]
        for v in reads:
            for k in v.keys:
                c = self.cells.setdefault(k, [None, {}])
                if c[1].get(ev[0], 0) < ev[1]:
                    c[1][ev[0]] = ev[1]

    def op(self, eng, fn, reads=(), writes=(), signal=True):
        self._emit_waits(eng, self._deps(eng, reads, writes))
        if signal:
            self.cnt[eng] += 1
            ev = (eng, self.cnt[eng])
            sem = self.semh[eng]
            self.q[eng].append(lambda e, fn=fn, sem=sem: fn(e).then_inc(sem, 1))
            self.tr[eng].append(("i", eng, 1))
            self.pending[eng] = False
        else:
            ev = (eng, self.cnt[eng] + 1)
            self.q[eng].append(lambda e, fn=fn: fn(e))
            self.pending[eng] = True
        self._record(ev, reads, writes)
        return ev

    def dma(self, qe, out, in_, reads=None, writes=None):
        reads = [in_] if reads is None else reads
        writes = [out] if writes is None else writes
        deps = self._deps(qe, reads, writes)
        i = self.dma_i[qe]
        self.dma_i[qe] += 1
        sk = ("d", qe, i % self.NDS)
        val = 16 * (i // self.NDS + 1)
        if i >= self.NDS and deps.get(sk, 0) < val - 16:
            deps[sk] = val - 16
        self._emit_waits(qe, deps)
        sem = self.semh[sk]
        oap, iap = out.ap, in_.ap

        def f(e, oap=oap, iap=iap, sem=sem):
            o = oap(e) if callable(oap) else oap
            s = iap(e) if callable(iap) else iap
            try:
                e.dma_start(out=o, in_=s).then_inc(sem, 16)
            except Exception:
                print("DMA FAIL", o, s)
                raise

        self.q[qe].append(f)
        self.tr[qe].append(("i", sk, 16))
        ev = (sk, val)
        self._record(ev, reads, writes)
        return ev

    def cc(self, in_v, out_v, groups):
        deps = self._deps("pool", [in_v], [out_v])
        self._emit_waits("pool", deps)
        sk = ("cc", self.ncc)
        self.ncc += 1
        self.semh[sk] = self.es.enter_context(self.nc.semaphore("cc%d" % self.ncc))
        sem = self.semh[sk]
        iap, oap = in_v.ap, out_v.ap
        self.q["pool"].append(
            lambda e: e.collective_compute(
                "AllGather", ALU.bypass, replica_groups=groups, ins=[iap.opt()], outs=[oap.opt()]
            ).then_inc(sem)
        )
        self.tr["pool"].append(("i", sk, 1))
        ev = (sk, 1)
        self._record(ev, [in_v], [out_v])
        return ev

    def check_deadlock(self):
        pos = {e: 0 for e in ENGS}
        sem = {}
        progress = True
        while progress:
            progress = False
            for e in ENGS:
                tr = self.tr[e]
                while pos[e] < len(tr):
                    k, sk, val = tr[pos[e]]
                    if k == "w":
                        if sem.get(sk, 0) < val:
                            break
                    else:
                        sem[sk] = sem.get(sk, 0) + val
                    pos[e] += 1
                    progress = True
        stuck = {e: (pos[e], len(self.tr[e]), self.tr[e][pos[e]], sem.get(self.tr[e][pos[e]][1], 0))
                 for e in ENGS if pos[e] < len(self.tr[e])}
        assert not stuck, "DEADLOCK: %r" % (stuck,)

    def wait_all(self, eng, views):
        self._emit_waits(eng, self._deps(eng, views, []))

    def mm(self, out, lhsT, rhs, start, stop, signal=None, skip=False):
        if signal is None:
            signal = stop
        o, l, r = out.ap, lhsT.ap, rhs.ap
        if skip:
            fn = lambda e: e.matmul(o, l, r, start=start, stop=stop, skip_group_check=True)
        else:
            fn = lambda e: e.matmul(o, l, r, start=start, stop=stop)
        return self.op("pe", fn, reads=[lhsT, rhs], writes=[out], signal=signal)

    def act(self, out, in_, func, bias=None, scale=None, extra_reads=()):
        o, i = out.ap, in_.ap
        kw = {}
        if bias is not None:
            kw["bias"] = bias
        if scale is not None:
            kw["scale"] = scale
        return self.op("act", lambda e: e.activation(o, i, func, **kw),
                       reads=[in_] + list(extra_reads), writes=[out])

    def tt(self, eng, out, a, b, op):
        o, x, y = out.ap, a.ap, b.ap
        return self.op(eng, lambda e: e.tensor_tensor(o, x, y, op), reads=[a, b], writes=[out])

    def ts(self, eng, out, a, s1, s2, op0, op1=None, extra_reads=()):
        o, x = out.ap, a.ap
        if op1 is None:
            fn = lambda e: e.tensor_scalar(o, x, s1, s2, op0)
        else:
            fn = lambda e: e.tensor_scalar(o, x, s1, s2, op0, op1)
        return self.op(eng, fn, reads=[a] + list(extra_reads), writes=[out])

    def stt(self, eng, out, a, s, b, op0, op1, extra_reads=()):
        o, x, y = out.ap, a.ap, b.ap
        return self.op(eng, lambda e: e.scalar_tensor_tensor(o, x, s, y, op0, op1),
                       reads=[a, b] + list(extra_reads), writes=[out])

    def copy(self, eng, out, in_):
        o, i = out.ap, in_.ap
        return self.op(eng, (lambda e: e.copy(o, i)) if eng == "act" else (lambda e: e.tensor_copy(o, i)), reads=[in_], writes=[out])

    def recip(self, eng, out, in_):
        o, i = out.ap, in_.ap
        return self.op(eng, lambda e: e.reciprocal(o, i), reads=[in_], writes=[out])

    def memset(self, eng, out, val):
        o = out.ap
        return self.op(eng, lambda e: e.memset(o, val), writes=[out])


def piece_shapes():
    ffn_p = []
    f0 = 0
    while f0 < 44:
        nf = min(8, 44 - f0)
        ffn_p += [(D, 2 * nf * 128), (nf * 128, D)]
        f0 += nf
    return ffn_p + [(D, 2048)] * 3 + [(1024, D)] * 2 + ffn_p


def make_pieces(w1_gu, w1_down, w_in, w_out, w2_gu, w2_down):
    def ffn_p(wgu, wdn):
        out = []
        f0 = 0
        while f0 < 44:
            nf = min(8, 44 - f0)
            a, b = f0 * 128, (f0 + nf) * 128
            out.append(np.concatenate([wgu[:, a:b], wgu[:, DFF + a:DFF + b]], axis=1))
            out.append(wdn[a:b, :])
            f0 += nf
        return out
    cols = []
    for blk in range(4):
        for base in (3072, 4096, 5120):
            cols.append(np.arange(base + blk * 256, base + blk * 256 + 256))
    cols += [np.arange(0, 1024), np.arange(1024, 2048), np.arange(2048, 3072)]
    winp = w_in[:, np.concatenate(cols)]
    ps = ffn_p(w1_gu, w1_down) + [winp[:, 0:2048], winp[:, 2048:4096], winp[:, 4096:6144]]
    ps += [w_out[1024:2048, :], w_out[0:1024, :]] + ffn_p(w2_gu, w2_down)
    return ps


def build_program():
    nc = bass.Bass("TRN2", target_bir_lowering=False)
    es = ExitStack()
    with es:
        es.enter_context(nc.allow_low_precision("bf16 matmul operands, fp32 accumulation"))
        kb = KB(nc, es)

        def din(name, shape, dt=F32):
            return nc.dram_tensor(name, shape, dt, kind="ExternalInput").ap()

        xT_d = din("xT", [D, NT])
        cT_d = din("cT", [D, 2])
        wada_d = din("w_ada", [D, 2304])
        badaT_d = din("b_adaT", [128, 18])
        qg_d = din("q_g", [128, 1])
        kg_d = din("k_g", [128, 1])
        cw_d = din("conv_wT", [128, 24])
        consts_d = din("consts", [128, 384], BF16)
        outT_d = nc.dram_tensor("outT", [D, NT], F32, kind="ExternalOutput").ap()

        def dint(name, shape, dt):
            return nc.dram_tensor(name, shape, dt, kind="Internal").ap()

        mod_in = dint("mod_in", [128, 36], F32)
        mod_all = dint("mod_all", [1024, 36], F32)
        halo_in = dint("halo_in", [1024, 2], F32)
        halo_all = dint("halo_all", [8192, 2], F32)
        hbuf = dint("hbuf", [8192, 2], F32)
        q_in = dint("q_in", [1024, NT], BF16)
        q_all = dint("q_all", [8192, NT], BF16)
        k_in = dint("k_in", [1024, NT], BF16)
        k_all = dint("k_all", [8192, NT], BF16)
        v_in = dint("v_in", [1024, 1024], BF16)
        v_all = dint("v_all", [8192, 1024], BF16)
        att_in = dint("att_in", [256, SEQ], BF16)
        att_all = dint("att_all", [2048, SEQ], BF16)

        def dreg(name, ap, ncell=1):
            return Reg(name, ap, ncell, 1)

        def sb(name, n, dt):
            return es.enter_context(nc.sbuf_tensor(name, [128, n], dt))

        X = Reg("X", sb("X", NFT * NT, F32), NFT * NT, 512)
        H = Reg("H", sb("H", 16384, BF16), 16384, 512)
        A = Reg("A", sb("A", 8192, BF16), 8192, 512)
        TF = Reg("TF", sb("TF", 5120, F32), 5120, 512)
        TB = Reg("TB", sb("TB", 5120, BF16), 5120, 512)
        W = [Reg("W%d" % i, sb("W%d" % i, 4096, BF16), 4096, 4096) for i in range(NSLOT)]
        MODT = Reg("MODT", sb("MODT", 144, F32), 144, 144)
        MD = Reg("MD", sb("MD", 96, F32), 96, 16)
        CON = Reg("CON", sb("CON", 384, BF16), 384, 384)
        SM = Reg("SM", sb("SM", 256, F32), 256, 8)
        CB = Reg("CB", sb("CB", 32, BF16), 32, 32)
        PS = [Reg("PS%d" % i, es.enter_context(nc.psum_tensor("PS%d" % i, [128, 512], F32)), 512, 128)
              for i in range(8)]

        ones = CON.v(0, 128)
        tri1 = CON.v(128, 256)
        tri2 = CON.v(256, 384)
        cT_sb = SM.v(0, 32)
        modout = SM.v(32, 68)
        badaT = SM.v(68, 86)
        qg = SM.v(86, 87)
        kg = SM.v(87, 88)
        cw = SM.v(88, 112)
        halo_out = SM.v(112, 128)
        cu01 = SM.v(128, 144)
        gb01 = SM.v(144, 160)
        halo_sb = SM.v(160, 176)
        y01 = SM.v(176, 192)
        tmp16 = SM.v(192, 208)
        zeros16 = SM.v(216, 232)

        R_modin = dreg("mod_in", mod_in)
        R_modall = dreg("mod_all", mod_all)
        R_haloin = dreg("halo_in", halo_in)
        R_haloall = dreg("halo_all", halo_all)
        R_hbuf = dreg("hbuf", hbuf, 4)
        R_qin = dreg("q_in", q_in, 16)
        R_qall = dreg("q_all", q_all)
        R_kin = dreg("k_in", k_in, 16)
        R_kall = dreg("k_all", k_all)
        R_vin = dreg("v_in", v_in, 32)
        R_vall = dreg("v_all", v_all)
        R_attin = dreg("att_in", att_in, 16)
        R_attall = dreg("att_all", att_all)
        R_out = dreg("outT", outT_d)

        def dv(reg, ap, lo=0, hi=None):
            hi = reg.n if hi is None else hi
            return V(ap, reg.keys(lo, hi))

        pid_cache = {}

        def pid(e):
            k = id(e)
            if k not in pid_cache:
                pid_cache[k] = e.partition_id()
            return pid_cache[k]

        wstate = {"i": 0}

        pieces = []
        for (pk, pn) in piece_shapes():
            i = len(pieces)
            ext = din("wp%d" % i, [pk // 8, pn])
            loc = dint("wl%d" % i, [pk // 8, pn], F32)
            full = dint("wf%d" % i, [pk, pn], F32)
            pieces.append(dict(ext=ext, loc=loc, full=full, Rl=dreg("wl%d" % i, loc), Rf=dreg("wf%d" % i, full)))
        pstate = {"n": 0}
        LOOK = 3

        def ensure(idx):
            while pstate["n"] < min(idx + 1 + LOOK, len(pieces)):
                p = pieces[pstate["n"]]
                pstate["n"] += 1
                kb.dma("sp", dv(p["Rl"], p["loc"]), V(p["ext"], []))
                kb.cc(dv(p["Rl"], p["loc"]), dv(p["Rf"], p["full"]), G8)

        def wload(src, kt, ncol):
            s = W[wstate["i"] % NSLOT]
            wstate["i"] += 1
            dst = s.v(0, kt * ncol, "p (t n) -> p t n", t=kt)
            if isinstance(src, tuple):
                pi, r0, r1, c0, c1 = src
                ensure(pi)
                p = pieces[pi]
                sv = dv(p["Rf"], p["full"][r0:r1, c0:c1].rearrange("(t p) n -> p t n", p=128))
            else:
                sv = V(src.rearrange("(t p) n -> p t n", p=128), [])
            kb.dma("pool", dst, sv)
            return s, dst

        def wslice(s, kt_i, ncol, c0, c1):
            return s.v(kt_i * ncol + c0, kt_i * ncol + c1)

        kb.dma("sp", CON.v(0, 384), V(consts_d, []))
        kb.dma("sp", SM.v(0, 32, "p (kt b) -> p kt b", b=2), V(cT_d.rearrange("(kt p) b -> p kt b", p=128), []))
        kb.dma("sp", badaT, V(badaT_d, []))
        kb.dma("sp", qg, V(qg_d, []))
        kb.dma("sp", kg, V(kg_d, []))
        kb.dma("sp", cw, V(cw_d, []))
        for ft in range(NFT):
            kb.dma("sp", X.v(ft * NT, (ft + 1) * NT), V(xT_d[ft * 128:(ft + 1) * 128, :], []))
        kb.memset("dve", SM.v(208, 209), EPS)
        kb.memset("dve", zeros16, 0.0)
        kb.dma("sp", dv(R_hbuf, hbuf[0:1024, :].rearrange("(ct p) k -> p ct k", p=128), 0, 1),
               SM.v(216, 232, "p (ct k) -> p ct k", k=2))
        kb.dma("sp", dv(R_hbuf, hbuf[4096:5120, :].rearrange("(ct p) k -> p ct k", p=128), 1, 2),
               SM.v(216, 232, "p (ct k) -> p ct k", k=2))
        eps_ap = SM.t[:, 208:209]

        kb.act(CB.v(0, 32), cT_sb, AF.Silu)
        mps = PS[4]
        for blk in range(9):
            s, _ = wload(wada_d[:, blk * 256:(blk + 1) * 256], 16, 256)
            for h in range(2):
                t = blk * 2 + h
                for kt in range(16):
                    kb.mm(mps.v(2 * t, 2 * t + 2), wslice(s, kt, 256, h * 128, h * 128 + 128),
                          CB.v(2 * kt, 2 * kt + 2), start=(kt == 0), stop=(kt == 15))
        for b in range(2):
            src = V(mps.t[:, 0:36].rearrange("p (t b) -> p t b", b=2)[:, :, b], mps.keys(0, 36))
            kb.tt("dve", SM.v(32 + 18 * b, 50 + 18 * b), src, badaT, ALU.add)
        kb.dma("sp", dv(R_modin, mod_in), modout)
        kb.cc(dv(R_modin, mod_in), dv(R_modall, mod_all), G8)

        def mod_src(e):
            b = pid(e) // 4
            return mod_all[:, bass.ds(b * 18, 18)].rearrange("(r p) t -> p r t", p=128)

        kb.dma("sp", MODT.v(0, 144, "p (r t) -> p r t", r=8), dv(R_modall, mod_src))
        for i, m in enumerate((1, 4, 7)):
            kb.ts("dve", MD.v(16 * i, 16 * i + 16), MODT.v(16 * m, 16 * m + 16), 1.0, None, ALU.add)
        kb.ts("dve", MD.v(48, 64), MODT.v(32, 48), 0.5, None, ALU.mult)
        kb.ts("dve", MD.v(64, 80), MODT.v(80, 96), 1.0, None, ALU.mult)
        kb.ts("dve", MD.v(80, 96), MODT.v(128, 144), 0.5, None, ALU.mult)

        def xv(ft, c0=0, c1=NT):
            return X.v(ft * NT + c0, ft * NT + c1)

        def hv(ft, c0=0, c1=NT):
            return H.v(ft * NT + c0, ft * NT + c1)

        rstd = TF.v(0, 1024)

        def norm_mod(shcol, sccol):
            for ft in range(NFT):
                sq = TB.v((ft % 2) * 1024, (ft % 2) * 1024 + 1024)
                kb.act(sq, xv(ft), AF.Square)
                for c in range(2):
                    kb.mm(PS[c].v(0, 512), ones, TB.v((ft % 2) * 1024 + c * 512, (ft % 2) * 1024 + c * 512 + 512),
                          start=(ft == 0), stop=(ft == NFT - 1), signal=True)
            for c in range(2):
                r = TF.v(c * 512, c * 512 + 512)
                kb.act(r, PS[c].v(0, 512), AF.Sqrt, bias=eps_ap, scale=1.0 / D, extra_reads=[SM.v(208, 209)])
                kb.recip("dve", r, r)
            for ft in range(NFT):
                tmpx = TF.v(1024 + (ft % 2) * 1024, 2048 + (ft % 2) * 1024)
                kb.tt("dve", tmpx, xv(ft), rstd, ALU.mult)
                kb.act(hv(ft), tmpx, AF.Identity,
                       bias=MODT.t[:, shcol + ft:shcol + ft + 1], scale=MD.t[:, sccol + ft:sccol + ft + 1],
                       extra_reads=[MODT.v(shcol, shcol + 16), MD.v(sccol, sccol + 16)])

        def ffn(pbase, gcol):
            gu_i = 0
            dn_i = 0
            f0 = 0
            sp_i = -1
            while f0 < 44:
                nf = min(8, 44 - f0)
                sp_i += 1
                wc = nf * 128
                for pr in range(nf // 2):
                    fb = f0 + pr * 2
                    sg, _ = wload((pbase + 2 * sp_i, 0, D, pr * 256, pr * 256 + 256), 16, 256)
                    su, _ = wload((pbase + 2 * sp_i, 0, D, wc + pr * 256, wc + pr * 256 + 256), 16, 256)
                    for h in range(2):
                        fl = pr * 2 + h
                        for c in range(2):
                            bset = gu_i % 2
                            gu_i += 1
                            pg, pu = PS[2 * bset], PS[2 * bset + 1]
                            for kt in range(16):
                                kb.mm(pg.v(0, 512), wslice(sg, kt, 256, h * 128, h * 128 + 128),
                                      hv(kt, c * 512, c * 512 + 512), start=(kt == 0), stop=(kt == 15))
                            for kt in range(16):
                                kb.mm(pu.v(0, 512), wslice(su, kt, 256, h * 128, h * 128 + 128),
                                      hv(kt, c * 512, c * 512 + 512), start=(kt == 0), stop=(kt == 15))
                            stmp = TF.v(3072 + bset * 512, 3584 + bset * 512)
                            kb.act(stmp, pg.v(0, 512), AF.Silu)
                            kb.tt("dve", A.v(fl * NT + c * 512, fl * NT + c * 512 + 512), stmp, pu.v(0, 512), ALU.mult)
                for dg in range(4):
                    sd, _ = wload((pbase + 2 * sp_i + 1, 0, nf * 128, dg * 512, (dg + 1) * 512), nf, 512)
                    for dl in range(4):
                        dt_ = dg * 4 + dl
                        bset = dn_i % 2
                        dn_i += 1
                        for c in range(2):
                            pd = PS[4 + 2 * bset + c]
                            for fl in range(nf):
                                kb.mm(pd.v(0, 512), wslice(sd, fl, 512, dl * 128, dl * 128 + 128),
                                      A.v(fl * NT + c * 512, fl * NT + c * 512 + 512),
                                      start=(fl == 0), stop=(fl == nf - 1))
                            xs = xv(dt_, c * 512, c * 512 + 512)
                            kb.stt("dve", xs, pd.v(0, 512), MD.t[:, gcol + dt_:gcol + dt_ + 1], xs,
                                   ALU.mult, ALU.add, extra_reads=[MD.v(gcol, gcol + 16)])
                f0 += nf

        if STAGE >= 1 and DO_FFN1:
            norm_mod(0, 0)
            ffn(0, 48)

        if STAGE >= 2:
            mixer(kb, locals())
        if STAGE >= 3:
            g_ = locals()
            norm_mod(96, 32)
            ffn(17, 80)

        ev = kb.dma("sp", dv(R_out, outT_d.rearrange("(ft p) t -> p ft t", p=128)),
                    X.v(0, NFT * NT, "p (ft t) -> p ft t", ft=NFT))
        kb.wait_all("sp", [dv(R_out, None)])
        for e in ENGS:
            assert not kb.pending[e], e
        kb.check_deadlock()

        with nc.Block() as block:
            @block.tensor
            def _(e):
                for f in kb.q["pe"]:
                    f(e)

            @block.scalar
            def _(e):
                for f in kb.q["act"]:
                    f(e)

            @block.vector
            def _(e):
                for f in kb.q["dve"]:
                    f(e)

            @block.gpsimd
            def _(e):
                for f in kb.q["pool"]:
                    f(e)

            @block.sync
            def _(e):
                for f in kb.q["sp"]:
                    f(e)
    return nc


def mixer(kb, L):
    g = dict(L)
    X, H, A, TF, TB, PS, SM, MD, MODT = (g[k] for k in ("X", "H", "A", "TF", "TB", "PS", "SM", "MD", "MODT"))
    ones, tri1, tri2 = g["ones"], g["tri1"], g["tri2"]
    wload, wslice, xv, hv, dv, pid = g["wload"], g["wslice"], g["xv"], g["hv"], g["dv"], g["pid"]
    winblk = lambda bi: (12 + bi // 8, 0, D, (bi % 8) * 256, (bi % 8) * 256 + 256)
    qg, kg, cw = g["qg"], g["kg"], g["cw"]
    halo_out, cu01, gb01, halo_sb, y01, tmp16 = (g[k] for k in ("halo_out", "cu01", "gb01", "halo_sb", "y01", "tmp16"))

    g["norm_mod"](48, 16)

    cu = TF.v(1024, 2048)
    yv = TF.v(2048, 3072)
    gb = TF.v(3072, 4096)
    it = 0
    for blk in range(4):
        sb_, _ = wload(winblk(blk * 3), 16, 256)
        sc_, _ = wload(winblk(blk * 3 + 1), 16, 256)
        su_, _ = wload(winblk(blk * 3 + 2), 16, 256)
        for h in range(2):
            ct = blk * 2 + h
            for c in range(2):
                bset = it % 2
                it += 1
                pb, pc, pu = PS[3 * bset], PS[3 * bset + 1], PS[3 * bset + 2]
                for (pp, ss) in ((pb, sb_), (pc, sc_), (pu, su_)):
                    for kt in range(16):
                        kb.mm(pp.v(0, 512), wslice(ss, kt, 256, h * 128, h * 128 + 128),
                              hv(kt, c * 512, c * 512 + 512), start=(kt == 0), stop=(kt == 15))
                utmp = TF.v(4096 + bset * 512, 4608 + bset * 512)
                kb.copy("act", utmp, pu.v(0, 512))
                kb.tt("dve", TF.v(1024 + c * 512, 1536 + c * 512), pc.v(0, 512), utmp, ALU.mult)
                kb.copy("act", TF.v(3072 + c * 512, 3584 + c * 512), pb.v(0, 512))
            cwv = lambda k: SM.t[:, 88 + ct * 3 + k:88 + ct * 3 + k + 1]
            yo = TF.v(2048 + 2, 3072)
            kb.ts("dve", yo, TF.v(1024, 2046), cwv(0), None, ALU.mult, extra_reads=[cw])
            kb.stt("dve", yo, TF.v(1025, 2047), cwv(1), yo, ALU.mult, ALU.add, extra_reads=[cw])
            kb.stt("dve", yo, TF.v(1026, 2048), cwv(2), yo, ALU.mult, ALU.add, extra_reads=[cw])
            kb.tt("dve", A.v(ct * NT + 2, ct * NT + NT), TF.v(3072 + 2, 4096), yo, ALU.mult)
            kb.copy("dve", SM.v(112 + 2 * ct, 114 + 2 * ct), TF.v(2046, 2048))
            kb.copy("dve", SM.v(128 + 2 * ct, 130 + 2 * ct), TF.v(1024, 1026))
            kb.copy("dve", SM.v(144 + 2 * ct, 146 + 2 * ct), TF.v(3072, 3074))

    halo_in, halo_all, hbuf = g["halo_in"], g["halo_all"], g["hbuf"]
    R_haloin, R_haloall, R_hbuf = g["R_haloin"], g["R_haloall"], g["R_hbuf"]
    kb.dma("sp", dv(R_haloin, halo_in.rearrange("(ct p) k -> p ct k", p=128)),
           SM.v(112, 128, "p (ct k) -> p ct k", k=2))
    kb.cc(dv(R_haloin, halo_in), dv(R_haloall, halo_all), G8)
    kb.dma("sp", dv(R_hbuf, hbuf[1024:4096, :], 2, 3), dv(R_haloall, halo_all[0:3072, :]))
    kb.dma("sp", dv(R_hbuf, hbuf[5120:8192, :], 3, 4), dv(R_haloall, halo_all[4096:7168, :]))

    def halo_src(e):
        return hbuf[bass.ds(pid(e) * 1024, 1024), :].rearrange("(ct p) k -> p ct k", p=128)

    kb.dma("sp", SM.v(160, 176, "p (ct k) -> p ct k", k=2), dv(R_hbuf, halo_src, 0, 4))

    if MIX_STOP == 1:
        return
    q_in, k_in, v_in = g["q_in"], g["k_in"], g["v_in"]
    R_qin, R_kin, R_vin = g["R_qin"], g["R_kin"], g["R_vin"]
    it = 0
    for (col0, dst, Rdst, gain) in ((0, q_in, R_qin, 86), (1024, k_in, R_kin, 87)):
        for blk in range(4):
            s, _ = wload(winblk((12 if col0 == 0 else 16) + blk), 16, 256)
            for h in range(2):
                hd = blk * 2 + h
                for c in range(2):
                    bset = it % 2
                    it += 1
                    pq, pss = PS[6 + bset], PS[4 + bset]
                    for kt in range(16):
                        kb.mm(pq.v(0, 512), wslice(s, kt, 256, h * 128, h * 128 + 128),
                              hv(kt, c * 512, c * 512 + 512), start=(kt == 0), stop=(kt == 15))
                    sq = TB.v(bset * 512, bset * 512 + 512)
                    kb.act(sq, pq.v(0, 512), AF.Square)
                    kb.mm(pss.v(0, 512), ones, sq, start=True, stop=True)
                    r = TF.v(bset * 512, bset * 512 + 512)
                    kb.act(r, pss.v(0, 512), AF.Sqrt, bias=SM.t[:, 208:209], scale=1.0 / 128.0, extra_reads=[SM.v(208, 209)])
                    kb.recip("dve", r, r)
                    st = TB.v(1024 + (it % 4) * 512, 1536 + (it % 4) * 512)
                    kb.stt("dve", st, pq.v(0, 512), SM.t[:, gain:gain + 1], r, ALU.mult, ALU.mult,
                           extra_reads=[SM.v(gain, gain + 1)])
                    kb.dma("sp", dv(Rdst, dst[hd * 128:(hd + 1) * 128, c * 512:(c + 1) * 512], hd * 2 + c, hd * 2 + c + 1), st)
        if col0 == 0:
            kb.cc(dv(R_qin, q_in), dv(g["R_qall"], g["q_all"]), G8)
        else:
            kb.cc(dv(R_kin, k_in), dv(g["R_kall"], g["k_all"]), G8)
    if MIX_STOP == 11:
        return
    it = 0
    for blk in range(4):
        s, _ = wload(winblk(20 + blk), 16, 256)
        for tt in range(8):
            bset = it % 2
            it += 1
            pv = PS[6 + bset]
            for kt in range(16):
                kb.mm(pv.v(0, 256), hv(kt, tt * 128, tt * 128 + 128), wslice(s, kt, 256, 0, 256),
                      start=(kt == 0), stop=(kt == 15))
            st = TB.v(3072 + bset * 256, 3328 + bset * 256)
            kb.copy("act", st, pv.v(0, 256))
            dst = v_in[tt * 128:(tt + 1) * 128, blk * 256:(blk + 1) * 256]
            kb.dma("sp", dv(R_vin, dst, blk * 8 + tt, blk * 8 + tt + 1), st)
    kb.cc(dv(R_vin, v_in), dv(g["R_vall"], g["v_all"]), G8)

    if MIX_STOP == 12:
        return
    S = lambda a, b, **kw: SM.v(a, b, **kw)

    def s3(lo, k):
        return V(SM.t[:, lo:lo + 16].rearrange("p (ct k) -> p ct k", k=2)[:, :, k], SM.keys(lo, lo + 16))

    def cw3(k):
        return V(SM.t[:, 88:112].rearrange("p (ct k) -> p ct k", k=3)[:, :, k], SM.keys(88, 112))

    h0, h1, c0_, c1_ = s3(160, 0), s3(160, 1), s3(128, 0), s3(128, 1)
    y0, y1, t0 = s3(176, 0), s3(176, 1), SM.v(192, 200)
    kb.tt("dve", y0, h0, cw3(0), ALU.mult)
    kb.tt("dve", t0, h1, cw3(1), ALU.mult)
    kb.tt("dve", y0, y0, t0, ALU.add)
    kb.tt("dve", t0, c0_, cw3(2), ALU.mult)
    kb.tt("dve", y0, y0, t0, ALU.add)
    kb.tt("dve", y1, h1, cw3(0), ALU.mult)
    kb.tt("dve", t0, c0_, cw3(1), ALU.mult)
    kb.tt("dve", y1, y1, t0, ALU.add)
    kb.tt("dve", t0, c1_, cw3(2), ALU.mult)
    kb.tt("dve", y1, y1, t0, ALU.add)
    conv01 = V(A.t[:, 0:8192].rearrange("p (ct t) -> p ct t", ct=8)[:, :, 0:2], A.keys(0, 8192))
    kb.tt("dve", conv01, SM.v(176, 192, "p (ct k) -> p ct k", k=2), SM.v(144, 160, "p (ct k) -> p ct k", k=2), ALU.mult)

    if MIX_STOP == 13:
        return
    def wout_half(row0, src_fn):
        it2 = 0
        for dg in range(4):
            s, _ = wload((15 if row0 == 1024 else 16, 0, 1024, dg * 512, (dg + 1) * 512), 8, 512)
            for dl in range(4):
                dt_ = dg * 4 + dl
                bset = it2 % 2
                it2 += 1
                for c in range(2):
                    pd = PS[4 + 2 * bset + c]
                    for kt in range(8):
                        kb.mm(pd.v(0, 512), wslice(s, kt, 512, dl * 128, dl * 128 + 128),
                              src_fn(kt, c), start=(kt == 0), stop=(kt == 7))
                    xs = xv(dt_, c * 512, c * 512 + 512)
                    kb.stt("dve", xs, pd.v(0, 512), MD.t[:, 64 + dt_:64 + dt_ + 1], xs,
                           ALU.mult, ALU.add, extra_reads=[MD.v(64, 80)])

    wout_half(1024, lambda kt, c: A.v(kt * NT + c * 512, kt * NT + c * 512 + 512))

    if MIX_STOP == 2:
        return
    q_all, k_all, v_all = g["q_all"], g["k_all"], g["v_all"]
    R_qall, R_kall, R_vall = g["R_qall"], g["R_kall"], g["R_vall"]
    for e_ in range(2):
        def qsrc(e, e_=e_):
            return q_all.rearrange("(g r hd d) t -> g hd d r t", g=2, r=4, hd=8)[
                bass.ds(pid(e) // 4, 1), bass.ds((pid(e) % 4) * 2 + e_, 1)].rearrange("a o d r t -> (a o d) r t")

        def ksrc(e, e_=e_):
            return k_all.rearrange("(g r hd d) t -> g hd d r t", g=2, r=4, hd=8)[
                bass.ds(pid(e) // 4, 1), bass.ds((pid(e) % 4) * 2 + e_, 1)].rearrange("a o d r t -> (a o d) r t")

        def vsrc(e, e_=e_):
            return v_all[bass.ds((pid(e) // 4) * 4096, 4096), bass.ds((pid(e) % 4) * 256 + e_ * 128, 128)].rearrange(
                "(kb p) d -> p kb d", p=128)

        kb.dma("sp", H.v(e_ * 4096, e_ * 4096 + 4096, "d (r t) -> d r t", r=4), dv(R_qall, qsrc))
        kb.dma("sp", H.v(8192 + e_ * 4096, 8192 + e_ * 4096 + 4096, "d (r t) -> d r t", r=4), dv(R_kall, ksrc))
        kb.dma("sp", A.v(e_ * 4096, e_ * 4096 + 4096, "p (kb d) -> p kb d", d=128), dv(R_vall, vsrc))

    if MIX_STOP == 3:
        return
    att_in, att_all = g["att_in"], g["att_all"]
    R_attin, R_attall = g["R_attin"], g["R_attall"]
    SCALE = 1.0 / float(np.sqrt(128.0))
    for qc in range(8):
        nkb = 4 * qc + 4
        for step in range(nkb):
            kbk = nkb - 1 - step
            i = kbk - 4 * qc
            c0 = 128 * i if i > 0 else 0
            first = (step == 0)
            last = (step == nkb - 1)
            for ch in range(2):
                Z, Rb, O = PS[ch], PS[2 + ch], PS[4 + ch]
                par = (step % 2) * 2 + ch
                e_sb = TF.v(par * 512, par * 512 + 512)
                w_sb = TF.v(2048 + par * 512, 2560 + par * 512)
                L_sb = TB.v(par * 512, par * 512 + 512)
                A_sb = TB.v(2048 + par * 512, 2560 + par * 512)
                sl = lambda reg_v, base: reg_v(base + c0, base + 512)
                kT = H.v(8192 + ch * 4096 + kbk * 128, 8192 + ch * 4096 + kbk * 128 + 128)
                qT = H.v(ch * 4096 + qc * 512 + c0, ch * 4096 + qc * 512 + 512)
                kb.mm(Z.v(c0, 512), kT, qT, start=True, stop=True)
                kb.act(TF.v(par * 512 + c0, par * 512 + 512), Z.v(c0, 512), AF.Exp, scale=SCALE)
                kb.act(TB.v(par * 512 + c0, par * 512 + 512), TF.v(par * 512 + c0, par * 512 + 512), AF.Ln, bias=1.0)
                if i >= 0:
                    dL = TB.v(par * 512 + c0, par * 512 + c0 + 128)
                    kb.tt("dve", dL, dL, tri2, ALU.mult)
                Lv = TB.v(par * 512 + c0, par * 512 + 512)
                kb.mm(Rb.v(c0, 512), tri1, Lv, start=first, stop=False, signal=True, skip=True)
                kb.act(TF.v(2048 + par * 512 + c0, 2560 + par * 512), Rb.v(c0, 512), AF.Exp, scale=-1.0)
                if not last:
                    kb.mm(Rb.v(c0, 512), tri2, Lv, start=False, stop=False, signal=True, skip=True)
                Av = TB.v(2048 + par * 512 + c0, 2560 + par * 512)
                kb.tt("dve", Av, TF.v(par * 512 + c0, par * 512 + 512), TF.v(2048 + par * 512 + c0, 2560 + par * 512), ALU.mult)
                if i >= 0:
                    dA = TB.v(2048 + par * 512 + c0, 2048 + par * 512 + c0 + 128)
                    kb.tt("dve", dA, dA, tri2, ALU.mult)
                vS = A.v(ch * 4096 + kbk * 128, ch * 4096 + kbk * 128 + 128)
                kb.mm(O.v(c0, 512), vS, Av, start=first, stop=last, signal=True, skip=True)
        for ch in range(2):
            ost = TB.v(4096 + ch * 512, 4608 + ch * 512)
            kb.copy("act", ost, PS[4 + ch].v(0, 512))
            kb.dma("sp", dv(R_attin, att_in[ch * 128:(ch + 1) * 128, qc * 512:(qc + 1) * 512], qc * 2 + ch, qc * 2 + ch + 1), ost)
    kb.cc(dv(R_attin, att_in), dv(R_attall, att_all), G8)

    def att_src(e):
        return att_all[bass.ds((pid(e) // 4) * 1024, 1024), bass.ds((pid(e) % 4) * 1024, 1024)].rearrange(
            "(ft p) t -> p ft t", p=128)

    kb.dma("sp", H.v(0, 8192, "p (ft t) -> p ft t", ft=8), dv(R_attall, att_src))
    wout_half(0, lambda kt, c: H.v(kt * NT + c * 512, kt * NT + c * 512 + 512))


_NC_CACHE = {}


def _get_nc():
    if "nc" not in _NC_CACHE:
        _NC_CACHE["nc"] = build_program()
    return _NC_CACHE["nc"]


def kernel(x, c, w_ada, b_ada, w1_gu, w1_down, w_in, q_norm_w, k_norm_w, conv_w, w_out, w2_gu, w2_down):
    f32 = np.float32
    x = np.asarray(x, f32)
    c = np.asarray(c, f32)
    w_ada = np.asarray(w_ada, f32)[0]
    b_ada = np.asarray(b_ada, f32)[0]
    f32a = lambda a: np.asarray(a, f32)[0]
    pcs = make_pieces(f32a(w1_gu), f32a(w1_down), f32a(w_in), f32a(w_out), f32a(w2_gu), f32a(w2_down))
    for p_, (pk, pn) in zip(pcs, piece_shapes()):
        assert p_.shape == (pk, pn), (p_.shape, pk, pn)
    shared = {
        "cT": np.ascontiguousarray(c.T),
        "q_g": np.ascontiguousarray(np.asarray(q_norm_w, f32)[0].reshape(128, 1)),
        "k_g": np.ascontiguousarray(np.asarray(k_norm_w, f32)[0].reshape(128, 1)),
        "conv_wT": np.ascontiguousarray(
            np.asarray(conv_w, f32)[0].reshape(3, 8, 128).transpose(2, 1, 0).reshape(128, 24)),
    }
    j = np.arange(128)[:, None]
    s = np.arange(128)[None, :]
    consts = np.concatenate([np.ones((128, 128)), (j >= s), (j < s)], axis=1).astype(ml_dtypes.bfloat16)
    shared["consts"] = consts
    in_maps = []
    for i in range(8):
        b, jr = i // 4, i % 4
        m = dict(shared)
        m["xT"] = np.ascontiguousarray(x[b, jr * NT:(jr + 1) * NT, :].T)
        m["w_ada"] = np.ascontiguousarray(w_ada[:, i * 2304:(i + 1) * 2304])
        m["b_adaT"] = np.ascontiguousarray(b_ada[i * 2304:(i + 1) * 2304].reshape(18, 128).T)
        for pi, p_ in enumerate(pcs):
            kk = p_.shape[0] // 8
            m["wp%d" % pi] = np.ascontiguousarray(p_[i * kk:(i + 1) * kk, :])
        in_maps.append(m)
    nc = _get_nc()
    res = run_bass_kernel_spmd(nc, in_maps, core_ids=list(range(8)))
    out = np.empty((2, SEQ, D), f32)
    for i in range(8):
        b, jr = i // 4, i % 4
        out[b, jr * NT:(jr + 1) * NT, :] = res.results[i]["outT"].T
    return out
```

```python
import numpy as np
import ml_dtypes
from contextlib import ExitStack

import concourse.bass as bass
import concourse.mybir as mybir
from concourse.bass_utils import run_bass_kernel_spmd

F32 = mybir.dt.float32
BF16 = mybir.dt.bfloat16
AF = mybir.ActivationFunctionType
ALU = mybir.AluOpType

D = 2048
DFF = 5632
NT = 1024
SEQ = 4096
NFT = 16
EPS = 1e-6
G4 = [[0, 1, 2, 3], [4, 5, 6, 7]]
G8 = [list(range(8))]
ENGS = ["pe", "act", "dve", "pool", "sp"]
NSLOT = 7
STAGE = 99
DO_FFN1 = True
MIX_STOP = 99


class V:
    def __init__(self, ap, keys):
        self.ap = ap
        self.keys = keys


class Reg:
    def __init__(self, name, t, n, cell):
        self.name, self.t, self.n, self.cell = name, t, n, cell

    def keys(self, lo, hi):
        return [(self.name, c) for c in range(lo // self.cell, (hi - 1) // self.cell + 1)]

    def v(self, lo, hi, pat=None, p0=0, p1=128, **kw):
        ap = self.t[p0:p1, lo:hi]
        if pat is not None:
            ap = ap.rearrange(pat, **kw)
        return V(ap, self.keys(lo, hi))


class KB:
    def __init__(self, nc, es):
        self.nc, self.es = nc, es
        self.q = {e: [] for e in ENGS}
        self.cnt = {e: 0 for e in ENGS}
        self.pending = {e: False for e in ENGS}
        self.semh = {}
        self.waited = {e: {} for e in ENGS}
        self.cells = {}
        self.tr = {e: [] for e in ENGS}
        self.dma_i = {"sp": 0, "pool": 0, "act": 0}
        self.NDS = 8
        self.ncc = 0
        for e in ENGS:
            self.semh[e] = es.enter_context(nc.semaphore("s_" + e))
        for e in ("sp", "pool"):
            for i in range(self.NDS):
                self.semh[("d", e, i)] = es.enter_context(nc.semaphore("d_%s_%d" % (e, i)))

    def _deps(self, eng, reads, writes):
        deps = {}

        def need(ev):
            if ev is None:
                return
            sk, val = ev
            if eng == "pe" and sk == "pe":
                return
            if deps.get(sk, 0) < val:
                deps[sk] = val

        for v in reads:
            for k in v.keys:
                c = self.cells.get(k)
                if c:
                    need(c[0])
        for v in writes:
            for k in v.keys:
                c = self.cells.get(k)
                if c:
                    need(c[0])
                    for sk, val in c[1].items():
                        need((sk, val))
        return deps

    def _emit_waits(self, eng, deps):
        for sk, val in deps.items():
            if self.waited[eng].get(sk, 0) >= val:
                continue
            self.waited[eng][sk] = val
            sem = self.semh[sk]
            self.tr[eng].append(("w", sk, val))
            self.q[eng].append(lambda e, sem=sem, val=val: e.wait_ge(sem, val))

    def _record(self, ev, reads, writes):
        for v in writes:
            for k in v.keys:
                self.cells[k] = [ev, {}]
        for v in reads:
            for k in v.keys:
                c = self.cells.setdefault(k, [None, {}])
                if c[1].get(ev[0], 0) < ev[1]:
                    c[1][ev[0]] = ev[1]

    def op(self, eng, fn, reads=(), writes=(), signal=True):
        self._emit_waits(eng, self._deps(eng, reads, writes))
        if signal:
            self.cnt[eng] += 1
            ev = (eng, self.cnt[eng])
            sem = self.semh[eng]
            self.q[eng].append(lambda e, fn=fn, sem=sem: fn(e).then_inc(sem, 1))
            self.tr[eng].append(("i", eng, 1))
            self.pending[eng] = False
        else:
            ev = (eng, self.cnt[eng] + 1)
            self.q[eng].append(lambda e, fn=fn: fn(e))
            self.pending[eng] = True
        self._record(ev, reads, writes)
        return ev

    def dma(self, qe, out, in_, reads=None, writes=None):
        reads = [in_] if reads is None else reads
        writes = [out] if writes is None else writes
        deps = self._deps(qe, reads, writes)
        i = self.dma_i[qe]
        self.dma_i[qe] += 1
        sk = ("d", qe, i % self.NDS)
        val = 16 * (i // self.NDS + 1)
        if i >= self.NDS and deps.get(sk, 0) < val - 16:
            deps[sk] = val - 16
        self._emit_waits(qe, deps)
        sem = self.semh[sk]
        oap, iap = out.ap, in_.ap

        def f(e, oap=oap, iap=iap, sem=sem):
            o = oap(e) if callable(oap) else oap
            s = iap(e) if callable(iap) else iap
            try:
                e.dma_start(out=o, in_=s).then_inc(sem, 16)
            except Exception:
                print("DMA FAIL", o, s)
                raise

        self.q[qe].append(f)
        self.tr[qe].append(("i", sk, 16))
        ev = (sk, val)
        self._record(ev, reads, writes)
        return ev

    def cc(self, in_v, out_v, groups):
        deps = self._deps("pool", [in_v], [out_v])
        self._emit_waits("pool", deps)
        sk = ("cc", self.ncc)
        self.ncc += 1
        self.semh[sk] = self.es.enter_context(self.nc.semaphore("cc%d" % self.ncc))
        sem = self.semh[sk]
        iap, oap = in_v.ap, out_v.ap
        self.q["pool"].append(
            lambda e: e.collective_compute(
                "AllGather", ALU.bypass, replica_groups=groups, ins=[iap.opt()], outs=[oap.opt()]
            ).then_inc(sem)
        )
        self.tr["pool"].append(("i", sk, 1))
        ev = (sk, 1)
        self._record(ev, [in_v], [out_v])
        return ev

    def check_deadlock(self):
        pos = {e: 0 for e in ENGS}
        sem = {}
        progress = True
        while progress:
            progress = False
            for e in ENGS:
                tr = self.tr[e]
                while pos[e] < len(tr):
                    k, sk, val = tr[pos[e]]
                    if k == "w":
                        if sem.get(sk, 0) < val:
                            break
                    else:
                        sem[sk] = sem.get(sk, 0) + val
                    pos[e] += 1
                    progress = True
        stuck = {e: (pos[e], len(self.tr[e]), self.tr[e][pos[e]], sem.get(self.tr[e][pos[e]][1], 0))
                 for e in ENGS if pos[e] < len(self.tr[e])}
        assert not stuck, "DEADLOCK: %r" % (stuck,)

    def wait_all(self, eng, views):
        self._emit_waits(eng, self._deps(eng, views, []))

    def mm(self, out, lhsT, rhs, start, stop, signal=None, skip=False):
        if signal is None:
            signal = stop
        o, l, r = out.ap, lhsT.ap, rhs.ap
        if skip:
            fn = lambda e: e.matmul(o, l, r, start=start, stop=stop, skip_group_check=True)
        else:
            fn = lambda e: e.matmul(o, l, r, start=start, stop=stop)
        return self.op("pe", fn, reads=[lhsT, rhs], writes=[out], signal=signal)

    def act(self, out, in_, func, bias=None, scale=None, extra_reads=()):
        o, i = out.ap, in_.ap
        kw = {}
        if bias is not None:
            kw["bias"] = bias
        if scale is not None:
            kw["scale"] = scale
        return self.op("act", lambda e: e.activation(o, i, func, **kw),
                       reads=[in_] + list(extra_reads), writes=[out])

    def tt(self, eng, out, a, b, op):
        o, x, y = out.ap, a.ap, b.ap
        return self.op(eng, lambda e: e.tensor_tensor(o, x, y, op), reads=[a, b], writes=[out])

    def ts(self, eng, out, a, s1, s2, op0, op1=None, extra_reads=()):
        o, x = out.ap, a.ap
        if op1 is None:
            fn = lambda e: e.tensor_scalar(o, x, s1, s2, op0)
        else:
            fn = lambda e: e.tensor_scalar(o, x, s1, s2, op0, op1)
        return self.op(eng, fn, reads=[a] + list(extra_reads), writes=[out])

    def stt(self, eng, out, a, s, b, op0, op1, extra_reads=()):
        o, x, y = out.ap, a.ap, b.ap
        return self.op(eng, lambda e: e.scalar_tensor_tensor(o, x, s, y, op0, op1),
                       reads=[a, b] + list(extra_reads), writes=[out])

    def copy(self, eng, out, in_):
        o, i = out.ap, in_.ap
        return self.op(eng, (lambda e: e.copy(o, i)) if eng == "act" else (lambda e: e.tensor_copy(o, i)), reads=[in_], writes=[out])

    def recip(self, eng, out, in_):
        o, i = out.ap, in_.ap
        return self.op(eng, lambda e: e.reciprocal(o, i), reads=[in_], writes=[out])

    def memset(self, eng, out, val):
        o = out.ap
        return self.op(eng, lambda e: e.memset(o, val), writes=[out])


def piece_shapes():
    ffn_p = []
    f0 = 0
    while f0 < 44:
        nf = min(8, 44 - f0)
        ffn_p += [(D, 2 * nf * 128), (nf * 128, D)]
        f0 += nf
    return ffn_p + [(D, 2048)] * 3 + [(1024, D)] * 2 + ffn_p


def make_pieces(w1_gu, w1_down, w_in, w_out, w2_gu, w2_down):
    def ffn_p(wgu, wdn):
        out = []
        f0 = 0
        while f0 < 44:
            nf = min(8, 44 - f0)
            a, b = f0 * 128, (f0 + nf) * 128
            out.append(np.concatenate([wgu[:, a:b], wgu[:, DFF + a:DFF + b]], axis=1))
            out.append(wdn[a:b, :])
            f0 += nf
        return out
    cols = []
    for blk in range(4):
        for base in (3072, 4096, 5120):
            cols.append(np.arange(base + blk * 256, base + blk * 256 + 256))
    cols += [np.arange(0, 1024), np.arange(1024, 2048), np.arange(2048, 3072)]
    winp = w_in[:, np.concatenate(cols)]
    ps = ffn_p(w1_gu, w1_down) + [winp[:, 0:2048], winp[:, 2048:4096], winp[:, 4096:6144]]
    ps += [w_out[1024:2048, :], w_out[0:1024, :]] + ffn_p(w2_gu, w2_down)
    return ps


def build_program():
    nc = bass.Bass("TRN2", target_bir_lowering=False)
    es = ExitStack()
    with es:
        es.enter_context(nc.allow_low_precision("bf16 matmul operands, fp32 accumulation"))
        kb = KB(nc, es)

        def din(name, shape, dt=F32):
            return nc.dram_tensor(name, shape, dt, kind="ExternalInput").ap()

        xT_d = din("xT", [D, NT])
        cT_d = din("cT", [D, 2])
        wada_d = din("w_ada", [D, 2304])
        badaT_d = din("b_adaT", [128, 18])
        qg_d = din("q_g", [128, 1])
        kg_d = din("k_g", [128, 1])
        cw_d = din("conv_wT", [128, 24])
        consts_d = din("consts", [128, 384], BF16)
        outT_d = nc.dram_tensor("outT", [D, NT], F32, kind="ExternalOutput").ap()

        def dint(name, shape, dt):
            return nc.dram_tensor(name, shape, dt, kind="Internal").ap()

        mod_in = dint("mod_in", [128, 36], F32)
        mod_all = dint("mod_all", [1024, 36], F32)
        halo_in = dint("halo_in", [1024, 2], F32)
        halo_all = dint("halo_all", [8192, 2], F32)
        hbuf = dint("hbuf", [8192, 2], F32)
        q_in = dint("q_in", [1024, NT], BF16)
        q_all = dint("q_all", [8192, NT], BF16)
        k_in = dint("k_in", [1024, NT], BF16)
        k_all = dint("k_all", [8192, NT], BF16)
        v_in = dint("v_in", [1024, 1024], BF16)
        v_all = dint("v_all", [8192, 1024], BF16)
        att_in = dint("att_in", [256, SEQ], BF16)
        att_all = dint("att_all", [2048, SEQ], BF16)

        def dreg(name, ap, ncell=1):
            return Reg(name, ap, ncell, 1)

        def sb(name, n, dt):
            return es.enter_context(nc.sbuf_tensor(name, [128, n], dt))

        X = Reg("X", sb("X", NFT * NT, F32), NFT * NT, 512)
        H = Reg("H", sb("H", 16384, BF16), 16384, 512)
        A = Reg("A", sb("A", 8192, BF16), 8192, 512)
        TF = Reg("TF", sb("TF", 5120, F32), 5120, 512)
        TB = Reg("TB", sb("TB", 5120, BF16), 5120, 512)
        W = [Reg("W%d" % i, sb("W%d" % i, 4096, BF16), 4096, 4096) for i in range(NSLOT)]
        MODT = Reg("MODT", sb("MODT", 144, F32), 144, 144)
        MD = Reg("MD", sb("MD", 96, F32), 96, 16)
        CON = Reg("CON", sb("CON", 384, BF16), 384, 384)
        SM = Reg("SM", sb("SM", 256, F32), 256, 8)
        CB = Reg("CB", sb("CB", 32, BF16), 32, 32)
        PS = [Reg("PS%d" % i, es.enter_context(nc.psum_tensor("PS%d" % i, [128, 512], F32)), 512, 128)
              for i in range(8)]

        ones = CON.v(0, 128)
        tri1 = CON.v(128, 256)
        tri2 = CON.v(256, 384)
        cT_sb = SM.v(0, 32)
        modout = SM.v(32, 68)
        badaT = SM.v(68, 86)
        qg = SM.v(86, 87)
        kg = SM.v(87, 88)
        cw = SM.v(88, 112)
        halo_out = SM.v(112, 128)
        cu01 = SM.v(128, 144)
        gb01 = SM.v(144, 160)
        halo_sb = SM.v(160, 176)
        y01 = SM.v(176, 192)
        tmp16 = SM.v(192, 208)
        zeros16 = SM.v(216, 232)

        R_modin = dreg("mod_in", mod_in)
        R_modall = dreg("mod_all", mod_all)
        R_haloin = dreg("halo_in", halo_in)
        R_haloall = dreg("halo_all", halo_all)
        R_hbuf = dreg("hbuf", hbuf, 4)
        R_qin = dreg("q_in", q_in, 16)
        R_qall = dreg("q_all", q_all)
        R_kin = dreg("k_in", k_in, 16)
        R_kall = dreg("k_all", k_all)
        R_vin = dreg("v_in", v_in, 32)
        R_vall = dreg("v_all", v_all)
        R_attin = dreg("att_in", att_in, 16)
        R_attall = dreg("att_all", att_all)
        R_out = dreg("outT", outT_d)

        def dv(reg, ap, lo=0, hi=None):
            hi = reg.n if hi is None else hi
            return V(ap, reg.keys(lo, hi))

        pid_cache = {}

        def pid(e):
            k = id(e)
            if k not in pid_cache:
                pid_cache[k] = e.partition_id()
            return pid_cache[k]

        wstate = {"i": 0}

        pieces = []
        for (pk, pn) in piece_shapes():
            i = len(pieces)
            ext = din("wp%d" % i, [pk // 8, pn])
            loc = dint("wl%d" % i, [pk // 8, pn], F32)
            full = dint("wf%d" % i, [pk, pn], F32)
            pieces.append(dict(ext=ext, loc=loc, full=full, Rl=dreg("wl%d" % i, loc), Rf=dreg("wf%d" % i, full)))
        pstate = {"n": 0, "cap": 10 ** 9}
        LOOK = 3

        def ensure(idx):
            while pstate["n"] < min(idx + 1 + LOOK, len(pieces), pstate["cap"]):
                p = pieces[pstate["n"]]
                pstate["n"] += 1
                kb.dma("sp", dv(p["Rl"], p["loc"]), V(p["ext"], []))
                kb.cc(dv(p["Rl"], p["loc"]), dv(p["Rf"], p["full"]), G8)

        def wload(src, kt, ncol):
            s = W[wstate["i"] % NSLOT]
            wstate["i"] += 1
            dst = s.v(0, kt * ncol, "p (t n) -> p t n", t=kt)
            if isinstance(src, tuple):
                pi, r0, r1, c0, c1 = src
                ensure(pi)
                p = pieces[pi]
                sv = dv(p["Rf"], p["full"][r0:r1, c0:c1].rearrange("(t p) n -> p t n", p=128))
            else:
                sv = V(src.rearrange("(t p) n -> p t n", p=128), [])
            kb.dma("pool", dst, sv)
            return s, dst

        def wslice(s, kt_i, ncol, c0, c1):
            return s.v(kt_i * ncol + c0, kt_i * ncol + c1)

        kb.dma("sp", CON.v(0, 384), V(consts_d, []))
        kb.dma("sp", SM.v(0, 32, "p (kt b) -> p kt b", b=2), V(cT_d.rearrange("(kt p) b -> p kt b", p=128), []))
        kb.dma("sp", badaT, V(badaT_d, []))
        kb.dma("sp", qg, V(qg_d, []))
        kb.dma("sp", kg, V(kg_d, []))
        kb.dma("sp", cw, V(cw_d, []))
        for ft in range(NFT):
            kb.dma("sp", X.v(ft * NT, (ft + 1) * NT), V(xT_d[ft * 128:(ft + 1) * 128, :], []))
        kb.memset("dve", SM.v(208, 209), EPS)
        kb.memset("dve", zeros16, 0.0)
        kb.dma("sp", dv(R_hbuf, hbuf[0:1024, :].rearrange("(ct p) k -> p ct k", p=128), 0, 1),
               SM.v(216, 232, "p (ct k) -> p ct k", k=2))
        kb.dma("sp", dv(R_hbuf, hbuf[4096:5120, :].rearrange("(ct p) k -> p ct k", p=128), 1, 2),
               SM.v(216, 232, "p (ct k) -> p ct k", k=2))
        eps_ap = SM.t[:, 208:209]

        kb.act(CB.v(0, 32), cT_sb, AF.Silu)
        mps = PS[4]
        for blk in range(9):
            s, _ = wload(wada_d[:, blk * 256:(blk + 1) * 256], 16, 256)
            for h in range(2):
                t = blk * 2 + h
                for kt in range(16):
                    kb.mm(mps.v(2 * t, 2 * t + 2), wslice(s, kt, 256, h * 128, h * 128 + 128),
                          CB.v(2 * kt, 2 * kt + 2), start=(kt == 0), stop=(kt == 15))
        for b in range(2):
            src = V(mps.t[:, 0:36].rearrange("p (t b) -> p t b", b=2)[:, :, b], mps.keys(0, 36))
            kb.tt("dve", SM.v(32 + 18 * b, 50 + 18 * b), src, badaT, ALU.add)
        kb.dma("sp", dv(R_modin, mod_in), modout)
        kb.cc(dv(R_modin, mod_in), dv(R_modall, mod_all), G8)

        def mod_src(e):
            b = pid(e) // 4
            return mod_all[:, bass.ds(b * 18, 18)].rearrange("(r p) t -> p r t", p=128)

        kb.dma("sp", MODT.v(0, 144, "p (r t) -> p r t", r=8), dv(R_modall, mod_src))
        for i, m in enumerate((1, 4, 7)):
            kb.ts("dve", MD.v(16 * i, 16 * i + 16), MODT.v(16 * m, 16 * m + 16), 1.0, None, ALU.add)
        kb.ts("dve", MD.v(48, 64), MODT.v(32, 48), 0.5, None, ALU.mult)
        kb.ts("dve", MD.v(64, 80), MODT.v(80, 96), 1.0, None, ALU.mult)
        kb.ts("dve", MD.v(80, 96), MODT.v(128, 144), 0.5, None, ALU.mult)

        def xv(ft, c0=0, c1=NT):
            return X.v(ft * NT + c0, ft * NT + c1)

        def hv(ft, c0=0, c1=NT):
            return H.v(ft * NT + c0, ft * NT + c1)

        rstd = TF.v(0, 1024)

        def norm_mod(shcol, sccol):
            for ft in range(NFT):
                sq = TB.v((ft % 2) * 1024, (ft % 2) * 1024 + 1024)
                kb.act(sq, xv(ft), AF.Square)
                for c in range(2):
                    kb.mm(PS[c].v(0, 512), ones, TB.v((ft % 2) * 1024 + c * 512, (ft % 2) * 1024 + c * 512 + 512),
                          start=(ft == 0), stop=(ft == NFT - 1), signal=True)
            for c in range(2):
                r = TF.v(c * 512, c * 512 + 512)
                kb.act(r, PS[c].v(0, 512), AF.Sqrt, bias=eps_ap, scale=1.0 / D, extra_reads=[SM.v(208, 209)])
                kb.recip("dve", r, r)
            for ft in range(NFT):
                tmpx = TF.v(1024 + (ft % 2) * 1024, 2048 + (ft % 2) * 1024)
                kb.tt("dve", tmpx, xv(ft), rstd, ALU.mult)
                kb.act(hv(ft), tmpx, AF.Identity,
                       bias=MODT.t[:, shcol + ft:shcol + ft + 1], scale=MD.t[:, sccol + ft:sccol + ft + 1],
                       extra_reads=[MODT.v(shcol, shcol + 16), MD.v(sccol, sccol + 16)])

        def ffn(pbase, gcol):
            gu_i = 0
            dn_i = 0
            f0 = 0
            sp_i = -1
            while f0 < 44:
                nf = min(8, 44 - f0)
                sp_i += 1
                wc = nf * 128
                for pr in range(nf // 2):
                    fb = f0 + pr * 2
                    sg, _ = wload((pbase + 2 * sp_i, 0, D, pr * 256, pr * 256 + 256), 16, 256)
                    su, _ = wload((pbase + 2 * sp_i, 0, D, wc + pr * 256, wc + pr * 256 + 256), 16, 256)
                    for h in range(2):
                        fl = pr * 2 + h
                        for c in range(2):
                            bset = gu_i % 2
                            gu_i += 1
                            pg, pu = PS[2 * bset], PS[2 * bset + 1]
                            for kt in range(16):
                                kb.mm(pg.v(0, 512), wslice(sg, kt, 256, h * 128, h * 128 + 128),
                                      hv(kt, c * 512, c * 512 + 512), start=(kt == 0), stop=(kt == 15))
                            for kt in range(16):
                                kb.mm(pu.v(0, 512), wslice(su, kt, 256, h * 128, h * 128 + 128),
                                      hv(kt, c * 512, c * 512 + 512), start=(kt == 0), stop=(kt == 15))
                            stmp = TF.v(3072 + bset * 512, 3584 + bset * 512)
                            kb.act(stmp, pg.v(0, 512), AF.Silu)
                            kb.tt("dve", A.v(fl * NT + c * 512, fl * NT + c * 512 + 512), stmp, pu.v(0, 512), ALU.mult)
                for dg in range(4):
                    sd, _ = wload((pbase + 2 * sp_i + 1, 0, nf * 128, dg * 512, (dg + 1) * 512), nf, 512)
                    for dl in range(4):
                        dt_ = dg * 4 + dl
                        bset = dn_i % 2
                        dn_i += 1
                        for c in range(2):
                            pd = PS[4 + 2 * bset + c]
                            for fl in range(nf):
                                kb.mm(pd.v(0, 512), wslice(sd, fl, 512, dl * 128, dl * 128 + 128),
                                      A.v(fl * NT + c * 512, fl * NT + c * 512 + 512),
                                      start=(fl == 0), stop=(fl == nf - 1))
                            xs = xv(dt_, c * 512, c * 512 + 512)
                            kb.stt("dve", xs, pd.v(0, 512), MD.t[:, gcol + dt_:gcol + dt_ + 1], xs,
                                   ALU.mult, ALU.add, extra_reads=[MD.v(gcol, gcol + 16)])
                f0 += nf

        if STAGE >= 1 and DO_FFN1:
            norm_mod(0, 0)
            ffn(0, 48)

        if STAGE >= 2:
            mixer(kb, locals())
        if STAGE >= 3:
            g_ = locals()
            norm_mod(96, 32)
            ffn(17, 80)

        ev = kb.dma("sp", dv(R_out, outT_d.rearrange("(ft p) t -> p ft t", p=128)),
                    X.v(0, NFT * NT, "p (ft t) -> p ft t", ft=NFT))
        kb.wait_all("sp", [dv(R_out, None)])
        for e in ENGS:
            assert not kb.pending[e], e
        kb.check_deadlock()

        with nc.Block() as block:
            @block.tensor
            def _(e):
                for f in kb.q["pe"]:
                    f(e)

            @block.scalar
            def _(e):
                for f in kb.q["act"]:
                    f(e)

            @block.vector
            def _(e):
                for f in kb.q["dve"]:
                    f(e)

            @block.gpsimd
            def _(e):
                for f in kb.q["pool"]:
                    f(e)

            @block.sync
            def _(e):
                for f in kb.q["sp"]:
                    f(e)
    return nc


def mixer(kb, L):
    g = dict(L)
    X, H, A, TF, TB, PS, SM, MD, MODT = (g[k] for k in ("X", "H", "A", "TF", "TB", "PS", "SM", "MD", "MODT"))
    ones, tri1, tri2 = g["ones"], g["tri1"], g["tri2"]
    wload, wslice, xv, hv, dv, pid = g["wload"], g["wslice"], g["xv"], g["hv"], g["dv"], g["pid"]
    winblk = lambda bi: (12 + bi // 8, 0, D, (bi % 8) * 256, (bi % 8) * 256 + 256)
    qg, kg, cw = g["qg"], g["kg"], g["cw"]
    halo_out, cu01, gb01, halo_sb, y01, tmp16 = (g[k] for k in ("halo_out", "cu01", "gb01", "halo_sb", "y01", "tmp16"))

    g["norm_mod"](48, 16)
    g["pstate"]["cap"] = 16

    cu = TF.v(1024, 2048)
    yv = TF.v(2048, 3072)
    gb = TF.v(3072, 4096)
    it = 0
    for blk in range(4):
        sb_, _ = wload(winblk(blk * 3), 16, 256)
        sc_, _ = wload(winblk(blk * 3 + 1), 16, 256)
        su_, _ = wload(winblk(blk * 3 + 2), 16, 256)
        for h in range(2):
            ct = blk * 2 + h
            for c in range(2):
                bset = it % 2
                it += 1
                pb, pc, pu = PS[3 * bset], PS[3 * bset + 1], PS[3 * bset + 2]
                for (pp, ss) in ((pb, sb_), (pc, sc_), (pu, su_)):
                    for kt in range(16):
                        kb.mm(pp.v(0, 512), wslice(ss, kt, 256, h * 128, h * 128 + 128),
                              hv(kt, c * 512, c * 512 + 512), start=(kt == 0), stop=(kt == 15))
                utmp = TF.v(4096 + bset * 512, 4608 + bset * 512)
                kb.copy("act", utmp, pu.v(0, 512))
                kb.tt("dve", TF.v(1024 + c * 512, 1536 + c * 512), pc.v(0, 512), utmp, ALU.mult)
                kb.copy("act", TF.v(3072 + c * 512, 3584 + c * 512), pb.v(0, 512))
            cwv = lambda k: SM.t[:, 88 + ct * 3 + k:88 + ct * 3 + k + 1]
            yo = TF.v(2048 + 2, 3072)
            kb.ts("dve", yo, TF.v(1024, 2046), cwv(0), None, ALU.mult, extra_reads=[cw])
            kb.stt("dve", yo, TF.v(1025, 2047), cwv(1), yo, ALU.mult, ALU.add, extra_reads=[cw])
            kb.stt("dve", yo, TF.v(1026, 2048), cwv(2), yo, ALU.mult, ALU.add, extra_reads=[cw])
            kb.tt("dve", A.v(ct * NT + 2, ct * NT + NT), TF.v(3072 + 2, 4096), yo, ALU.mult)
            kb.copy("dve", SM.v(112 + 2 * ct, 114 + 2 * ct), TF.v(2046, 2048))
            kb.copy("dve", SM.v(128 + 2 * ct, 130 + 2 * ct), TF.v(1024, 1026))
            kb.copy("dve", SM.v(144 + 2 * ct, 146 + 2 * ct), TF.v(3072, 3074))

    halo_in, halo_all, hbuf = g["halo_in"], g["halo_all"], g["hbuf"]
    R_haloin, R_haloall, R_hbuf = g["R_haloin"], g["R_haloall"], g["R_hbuf"]
    kb.dma("sp", dv(R_haloin, halo_in.rearrange("(ct p) k -> p ct k", p=128)),
           SM.v(112, 128, "p (ct k) -> p ct k", k=2))
    kb.cc(dv(R_haloin, halo_in), dv(R_haloall, halo_all), G8)
    kb.dma("sp", dv(R_hbuf, hbuf[1024:4096, :], 2, 3), dv(R_haloall, halo_all[0:3072, :]))
    kb.dma("sp", dv(R_hbuf, hbuf[5120:8192, :], 3, 4), dv(R_haloall, halo_all[4096:7168, :]))

    def halo_src(e):
        return hbuf[bass.ds(pid(e) * 1024, 1024), :].rearrange("(ct p) k -> p ct k", p=128)

    kb.dma("sp", SM.v(160, 176, "p (ct k) -> p ct k", k=2), dv(R_hbuf, halo_src, 0, 4))

    if MIX_STOP == 1:
        return
    q_in, k_in, v_in = g["q_in"], g["k_in"], g["v_in"]
    R_qin, R_kin, R_vin = g["R_qin"], g["R_kin"], g["R_vin"]
    it = 0
    for (col0, dst, Rdst, gain) in ((0, q_in, R_qin, 86), (1024, k_in, R_kin, 87)):
        for blk in range(4):
            s, _ = wload(winblk((12 if col0 == 0 else 16) + blk), 16, 256)
            for h in range(2):
                hd = blk * 2 + h
                for c in range(2):
                    bset = it % 2
                    it += 1
                    pq, pss = PS[6 + bset], PS[4 + bset]
                    for kt in range(16):
                        kb.mm(pq.v(0, 512), wslice(s, kt, 256, h * 128, h * 128 + 128),
                              hv(kt, c * 512, c * 512 + 512), start=(kt == 0), stop=(kt == 15))
                    sq = TB.v(bset * 512, bset * 512 + 512)
                    kb.act(sq, pq.v(0, 512), AF.Square)
                    kb.mm(pss.v(0, 512), ones, sq, start=True, stop=True)
                    r = TF.v(bset * 512, bset * 512 + 512)
                    kb.act(r, pss.v(0, 512), AF.Sqrt, bias=SM.t[:, 208:209], scale=1.0 / 128.0, extra_reads=[SM.v(208, 209)])
                    kb.recip("dve", r, r)
                    st = TB.v(1024 + (it % 4) * 512, 1536 + (it % 4) * 512)
                    kb.stt("dve", st, pq.v(0, 512), SM.t[:, gain:gain + 1], r, ALU.mult, ALU.mult,
                           extra_reads=[SM.v(gain, gain + 1)])
                    kb.dma("sp", dv(Rdst, dst[hd * 128:(hd + 1) * 128, c * 512:(c + 1) * 512], hd * 2 + c, hd * 2 + c + 1), st)
        if col0 == 0:
            kb.cc(dv(R_qin, q_in), dv(g["R_qall"], g["q_all"]), G8)
        else:
            kb.cc(dv(R_kin, k_in), dv(g["R_kall"], g["k_all"]), G8)
    if MIX_STOP == 11:
        return
    it = 0
    for blk in range(4):
        s, _ = wload(winblk(20 + blk), 16, 256)
        for tt in range(8):
            bset = it % 2
            it += 1
            pv = PS[6 + bset]
            for kt in range(16):
                kb.mm(pv.v(0, 256), hv(kt, tt * 128, tt * 128 + 128), wslice(s, kt, 256, 0, 256),
                      start=(kt == 0), stop=(kt == 15))
            st = TB.v(3072 + bset * 256, 3328 + bset * 256)
            kb.copy("act", st, pv.v(0, 256))
            dst = v_in[tt * 128:(tt + 1) * 128, blk * 256:(blk + 1) * 256]
            kb.dma("sp", dv(R_vin, dst, blk * 8 + tt, blk * 8 + tt + 1), st)
    kb.cc(dv(R_vin, v_in), dv(g["R_vall"], g["v_all"]), G8)
    g["pstate"]["cap"] = 10 ** 9
    g["ensure"](16)

    if MIX_STOP == 12:
        return
    S = lambda a, b, **kw: SM.v(a, b, **kw)

    def s3(lo, k):
        return V(SM.t[:, lo:lo + 16].rearrange("p (ct k) -> p ct k", k=2)[:, :, k], SM.keys(lo, lo + 16))

    def cw3(k):
        return V(SM.t[:, 88:112].rearrange("p (ct k) -> p ct k", k=3)[:, :, k], SM.keys(88, 112))

    h0, h1, c0_, c1_ = s3(160, 0), s3(160, 1), s3(128, 0), s3(128, 1)
    y0, y1, t0 = s3(176, 0), s3(176, 1), SM.v(192, 200)
    kb.tt("dve", y0, h0, cw3(0), ALU.mult)
    kb.tt("dve", t0, h1, cw3(1), ALU.mult)
    kb.tt("dve", y0, y0, t0, ALU.add)
    kb.tt("dve", t0, c0_, cw3(2), ALU.mult)
    kb.tt("dve", y0, y0, t0, ALU.add)
    kb.tt("dve", y1, h1, cw3(0), ALU.mult)
    kb.tt("dve", t0, c0_, cw3(1), ALU.mult)
    kb.tt("dve", y1, y1, t0, ALU.add)
    kb.tt("dve", t0, c1_, cw3(2), ALU.mult)
    kb.tt("dve", y1, y1, t0, ALU.add)
    conv01 = V(A.t[:, 0:8192].rearrange("p (ct t) -> p ct t", ct=8)[:, :, 0:2], A.keys(0, 8192))
    kb.tt("dve", conv01, SM.v(176, 192, "p (ct k) -> p ct k", k=2), SM.v(144, 160, "p (ct k) -> p ct k", k=2), ALU.mult)

    if MIX_STOP == 13:
        return
    def wout_half(row0, src_fn):
        it2 = 0
        for dg in range(4):
            s, _ = wload((15 if row0 == 1024 else 16, 0, 1024, dg * 512, (dg + 1) * 512), 8, 512)
            for dl in range(4):
                dt_ = dg * 4 + dl
                bset = it2 % 2
                it2 += 1
                for c in range(2):
                    pd = PS[4 + 2 * bset + c]
                    for kt in range(8):
                        kb.mm(pd.v(0, 512), wslice(s, kt, 512, dl * 128, dl * 128 + 128),
                              src_fn(kt, c), start=(kt == 0), stop=(kt == 7))
                    xs = xv(dt_, c * 512, c * 512 + 512)
                    kb.stt("dve", xs, pd.v(0, 512), MD.t[:, 64 + dt_:64 + dt_ + 1], xs,
                           ALU.mult, ALU.add, extra_reads=[MD.v(64, 80)])

    wout_half(1024, lambda kt, c: A.v(kt * NT + c * 512, kt * NT + c * 512 + 512))

    if MIX_STOP == 2:
        return
    q_all, k_all, v_all = g["q_all"], g["k_all"], g["v_all"]
    R_qall, R_kall, R_vall = g["R_qall"], g["R_kall"], g["R_vall"]
    for e_ in range(2):
        def qsrc(e, e_=e_):
            return q_all.rearrange("(g r hd d) t -> g hd d r t", g=2, r=4, hd=8)[
                bass.ds(pid(e) // 4, 1), bass.ds((pid(e) % 4) * 2 + e_, 1)].rearrange("a o d r t -> (a o d) r t")

        def ksrc(e, e_=e_):
            return k_all.rearrange("(g r hd d) t -> g hd d r t", g=2, r=4, hd=8)[
                bass.ds(pid(e) // 4, 1), bass.ds((pid(e) % 4) * 2 + e_, 1)].rearrange("a o d r t -> (a o d) r t")

        def vsrc(e, e_=e_):
            return v_all[bass.ds((pid(e) // 4) * 4096, 4096), bass.ds((pid(e) % 4) * 256 + e_ * 128, 128)].rearrange(
                "(kb p) d -> p kb d", p=128)

        kb.dma("sp", H.v(e_ * 4096, e_ * 4096 + 4096, "d (r t) -> d r t", r=4), dv(R_qall, qsrc))
        kb.dma("sp", H.v(8192 + e_ * 4096, 8192 + e_ * 4096 + 4096, "d (r t) -> d r t", r=4), dv(R_kall, ksrc))
        kb.dma("sp", A.v(e_ * 4096, e_ * 4096 + 4096, "p (kb d) -> p kb d", d=128), dv(R_vall, vsrc))

    if MIX_STOP == 3:
        return
    att_in, att_all = g["att_in"], g["att_all"]
    R_attin, R_attall = g["R_attin"], g["R_attall"]
    SCALE = 1.0 / float(np.sqrt(128.0))
    for qc in range(8):
        nkb = 4 * qc + 4
        for step in range(nkb):
            kbk = nkb - 1 - step
            i = kbk - 4 * qc
            c0 = 128 * i if i > 0 else 0
            first = (step == 0)
            last = (step == nkb - 1)
            for ch in range(2):
                Z, Rb, O = PS[ch], PS[2 + ch], PS[4 + ch]
                par = (step % 2) * 2 + ch
                e_sb = TF.v(par * 512, par * 512 + 512)
                w_sb = TF.v(2048 + par * 512, 2560 + par * 512)
                L_sb = TB.v(par * 512, par * 512 + 512)
                A_sb = TB.v(2048 + par * 512, 2560 + par * 512)
                sl = lambda reg_v, base: reg_v(base + c0, base + 512)
                kT = H.v(8192 + ch * 4096 + kbk * 128, 8192 + ch * 4096 + kbk * 128 + 128)
                qT = H.v(ch * 4096 + qc * 512 + c0, ch * 4096 + qc * 512 + 512)
                kb.mm(Z.v(c0, 512), kT, qT, start=True, stop=True)
                kb.act(TF.v(par * 512 + c0, par * 512 + 512), Z.v(c0, 512), AF.Exp, scale=SCALE)
                kb.act(TB.v(par * 512 + c0, par * 512 + 512), TF.v(par * 512 + c0, par * 512 + 512), AF.Ln, bias=1.0)
                if i >= 0:
                    dL = TB.v(par * 512 + c0, par * 512 + c0 + 128)
                    kb.tt("dve", dL, dL, tri2, ALU.mult)
                Lv = TB.v(par * 512 + c0, par * 512 + 512)
                kb.mm(Rb.v(c0, 512), tri1, Lv, start=first, stop=False, signal=True, skip=True)
                kb.act(TF.v(2048 + par * 512 + c0, 2560 + par * 512), Rb.v(c0, 512), AF.Exp, scale=-1.0)
                if not last:
                    kb.mm(Rb.v(c0, 512), tri2, Lv, start=False, stop=False, signal=True, skip=True)
                Av = TB.v(2048 + par * 512 + c0, 2560 + par * 512)
                kb.tt("dve", Av, TF.v(par * 512 + c0, par * 512 + 512), TF.v(2048 + par * 512 + c0, 2560 + par * 512), ALU.mult)
                if i >= 0:
                    dA = TB.v(2048 + par * 512 + c0, 2048 + par * 512 + c0 + 128)
                    kb.tt("dve", dA, dA, tri2, ALU.mult)
                vS = A.v(ch * 4096 + kbk * 128, ch * 4096 + kbk * 128 + 128)
                kb.mm(O.v(c0, 512), vS, Av, start=first, stop=last, signal=True, skip=True)
        for ch in range(2):
            ost = TB.v(4096 + ch * 512, 4608 + ch * 512)
            kb.copy("act", ost, PS[4 + ch].v(0, 512))
            kb.dma("sp", dv(R_attin, att_in[ch * 128:(ch + 1) * 128, qc * 512:(qc + 1) * 512], qc * 2 + ch, qc * 2 + ch + 1), ost)
    kb.cc(dv(R_attin, att_in), dv(R_attall, att_all), G8)

    def att_src(e):
        return att_all[bass.ds((pid(e) // 4) * 1024, 1024), bass.ds((pid(e) % 4) * 1024, 1024)].rearrange(
            "(ft p) t -> p ft t", p=128)

    kb.dma("sp", H.v(0, 8192, "p (ft t) -> p ft t", ft=8), dv(R_attall, att_src))
    wout_half(0, lambda kt, c: H.v(kt * NT + c * 512, kt * NT + c * 512 + 512))


_NC_CACHE = {}


def _get_nc():
    if "nc" not in _NC_CACHE:
        _NC_CACHE["nc"] = build_program()
    return _NC_CACHE["nc"]


def kernel(x, c, w_ada, b_ada, w1_gu, w1_down, w_in, q_norm_w, k_norm_w, conv_w, w_out, w2_gu, w2_down):
    f32 = np.float32
    x = np.asarray(x, f32)
    c = np.asarray(c, f32)
    w_ada = np.asarray(w_ada, f32)[0]
    b_ada = np.asarray(b_ada, f32)[0]
    f32a = lambda a: np.asarray(a, f32)[0]
    pcs = make_pieces(f32a(w1_gu), f32a(w1_down), f32a(w_in), f32a(w_out), f32a(w2_gu), f32a(w2_down))
    for p_, (pk, pn) in zip(pcs, piece_shapes()):
        assert p_.shape == (pk, pn), (p_.shape, pk, pn)
    shared = {
        "cT": np.ascontiguousarray(c.T),
        "q_g": np.ascontiguousarray(np.asarray(q_norm_w, f32)[0].reshape(128, 1)),
        "k_g": np.ascontiguousarray(np.asarray(k_norm_w, f32)[0].reshape(128, 1)),
        "conv_wT": np.ascontiguousarray(
            np.asarray(conv_w, f32)[0].reshape(3, 8, 128).transpose(2, 1, 0).reshape(128, 24)),
    }
    j = np.arange(128)[:, None]
    s = np.arange(128)[None, :]
    consts = np.concatenate([np.ones((128, 128)), (j >= s), (j < s)], axis=1).astype(ml_dtypes.bfloat16)
    shared["consts"] = consts
    in_maps = []
    for i in range(8):
        b, jr = i // 4, i % 4
        m = dict(shared)
        m["xT"] = np.ascontiguousarray(x[b, jr * NT:(jr + 1) * NT, :].T)
        m["w_ada"] = np.ascontiguousarray(w_ada[:, i * 2304:(i + 1) * 2304])
        m["b_adaT"] = np.ascontiguousarray(b_ada[i * 2304:(i + 1) * 2304].reshape(18, 128).T)
        for pi, p_ in enumerate(pcs):
            kk = p_.shape[0] // 8
            m["wp%d" % pi] = np.ascontiguousarray(p_[i * kk:(i + 1) * kk, :])
        in_maps.append(m)
    nc = _get_nc()
    res = run_bass_kernel_spmd(nc, in_maps, core_ids=list(range(8)))
    out = np.empty((2, SEQ, D), f32)
    for i in range(8):
        b, jr = i // 4, i % 4
        out[b, jr * NT:(jr + 1) * NT, :] = res.results[i]["outT"].T
    return out
```
